# Optimizing a Trainium2 kernel written in Bass

```python
import math
import jax
import jax.numpy as jnp
from jax import lax
import numpy as np

D_MODEL = 1024
BATCH = 16
SEQ = 2048
DEPTH = 1

MEM_TOKENS = 256
HYENA_WIDTH = 512
HYENA_ORDER = 2
FILTER_EMB = 33
FILTER_HIDDEN = 64
FILTER_OUT_SCALE = 0.1
HYENA_TARGET = 1e-2
HYENA_FAST_DECAY_PCT = 0.3
HYENA_SLOW_DECAY_PCT = 1.5
HYENA_MIN_DECAY = math.log(HYENA_TARGET) / HYENA_FAST_DECAY_PCT
HYENA_MAX_DECAY = math.log(HYENA_TARGET) / HYENA_SLOW_DECAY_PCT
DIFF_HEADS = 4
DIFF_HEAD_DIM = 64
DIFF_WIDTH = DIFF_HEADS * 2 * DIFF_HEAD_DIM
MEM_HEADS = 4
MEM_HEAD_DIM = 128
MEM_WIDTH = MEM_HEADS * MEM_HEAD_DIM
MIX_WIDTH = HYENA_WIDTH + DIFF_WIDTH + MEM_WIDTH
IN_WIDTH = 3 * HYENA_WIDTH + 3 * DIFF_WIDTH + MEM_WIDTH
D_FF = 2816
SHORT_CONV = 3
ROPE_THETA = 10000.0
Q_BLOCK = 128
LN_EPS = 1e-5
RMS_EPS = 1e-5
DEEPNORM_ALPHA = (2.0 * DEPTH) ** 0.25
DEEPNORM_BETA = (8.0 * DEPTH) ** -0.25

kernel_name = 'hybrid_hyena_diffattn_memory_encoder'


def layer_norm(x, g, b):
    xf = x.astype(jnp.float32)
    mu = jnp.mean(xf, axis=-1, keepdims=True)
    xc = xf - mu
    var = jnp.mean(xc * xc, axis=-1, keepdims=True)
    return (xc * lax.rsqrt(var + LN_EPS) * g.astype(jnp.float32) + b.astype(jnp.float32)).astype(x.dtype)


def dwconv3(x, w, b):
    xp = jnp.pad(x, ((0, 0), (1, 1), (0, 0)))
    return xp[:, :-2] * w[0] + xp[:, 1:-1] * w[1] + xp[:, 2:] * w[2] + b


def rope_tables(seq_len, dim):
    inv_freq = ROPE_THETA ** (-jnp.arange(0, dim, 2, dtype=jnp.float32) / dim)
    ang = jnp.arange(seq_len, dtype=jnp.float32)[:, None] * inv_freq[None, :]
    ang = jnp.concatenate([ang, ang], axis=-1)
    return jnp.cos(ang), jnp.sin(ang)


def apply_rope(t, cos, sin):
    half = t.shape[-1] // 2
    tf = t.astype(jnp.float32)
    rot = jnp.concatenate([-tf[..., half:], tf[..., :half]], axis=-1)
    return (tf * cos + rot * sin).astype(t.dtype)


def hyena_filter_spectrum(seq_len, w1, b1, freq, w2, b2, w3):
    f32 = jnp.float32
    t = jnp.linspace(0.0, 1.0, seq_len, dtype=f32)[:, None]
    bands = (FILTER_EMB - 1) // 2
    fr = jnp.linspace(1e-4, bands - 1, bands, dtype=f32)[None, :]
    w = 2.0 * math.pi * jnp.arange(seq_len, dtype=f32)[:, None] / seq_len
    z = jnp.concatenate([t, jnp.cos(fr * w), -jnp.sin(fr * w)], axis=-1)
    freq = freq.astype(f32)
    h = jnp.sin(freq * (z @ w1.astype(f32) + b1.astype(f32)))
    h = jnp.sin(freq * (h @ w2.astype(f32) + b2.astype(f32)))
    h = h @ w3.astype(f32)
    deltas = jnp.abs(jnp.linspace(HYENA_MIN_DECAY, HYENA_MAX_DECAY, HYENA_WIDTH, dtype=f32))
    decay = jnp.exp(-t * deltas[None, :])
    h = h.reshape(seq_len, HYENA_ORDER, 2, HYENA_WIDTH) * decay[:, None, None, :]
    h_fwd, h_bwd = h[:, :, 0], h[:, :, 1]
    k = jnp.concatenate([h_fwd, jnp.zeros_like(h_fwd[:1]), h_bwd[:0:-1]], axis=0)
    return jnp.fft.rfft(k, axis=0)


def fft_long_conv(z, k_f, bias):
    L = z.shape[1]
    z_f = jnp.fft.rfft(z, n=2 * L, axis=1)
    y = jnp.fft.irfft(z_f * k_f[None], n=2 * L, axis=1)[:, :L]
    return y + z * bias


def hyena_mixer(u, conv_w, conv_b, k_f, bias):
    u = dwconv3(u, conv_w, conv_b).astype(jnp.float32)
    v, x1, x2 = jnp.split(u, 3, axis=-1)
    bias = bias.astype(jnp.float32)
    z = x1 * fft_long_conv(v, k_f[:, 0], bias[0])
    z = x2 * fft_long_conv(z, k_f[:, 1], bias[1])
    return z


def diff_attention(q, k, v, lam_params, subln_g, lambda_init):
    B, S = q.shape[0], q.shape[1]
    d = DIFF_HEAD_DIM
    cos, sin = rope_tables(S, d)
    q = apply_rope(jnp.transpose(q, (0, 2, 3, 1, 4)), cos, sin)
    k = apply_rope(jnp.transpose(k, (0, 2, 3, 1, 4)), cos, sin)
    v = jnp.transpose(v, (0, 2, 1, 3))
    lp = lam_params.astype(jnp.float32)
    lam = jnp.exp(jnp.sum(lp[0] * lp[1])) - jnp.exp(jnp.sum(lp[2] * lp[3])) + lambda_init
    scale = d ** -0.5
    n_blk = S // Q_BLOCK
    q_blocks = jnp.moveaxis(q.reshape(B, DIFF_HEADS, 2, n_blk, Q_BLOCK, d), 3, 0)

    def attend(q_blk):
        s = jnp.einsum('bhcqd,bhckd->bhcqk', q_blk, k).astype(jnp.float32) * scale
        p = jax.nn.softmax(s, axis=-1)
        a = p[:, :, 0] - lam * p[:, :, 1]
        return jnp.einsum('bhqk,bhke->bhqe', a.astype(v.dtype), v)

    o = lax.map(attend, q_blocks)
    o = jnp.moveaxis(o, 0, 2).reshape(B, DIFF_HEADS, S, 2 * d)
    of = o.astype(jnp.float32)
    of = of * lax.rsqrt(jnp.mean(of * of, axis=-1, keepdims=True) + RMS_EPS)
    of = of * subln_g.astype(jnp.float32) * (1.0 - lambda_init)
    return jnp.transpose(of, (0, 2, 1, 3)).reshape(B, S, DIFF_WIDTH).astype(v.dtype)


def memory_attention(q, mem, w_kv):
    B, S = q.shape[0], q.shape[1]
    M = mem.shape[1]
    q = q.reshape(B, S, MEM_HEADS, MEM_HEAD_DIM)
    kv = mem @ w_kv
    k, v = jnp.split(kv, 2, axis=-1)
    k = k.reshape(B, M, MEM_HEADS, MEM_HEAD_DIM)
    v = v.reshape(B, M, MEM_HEADS, MEM_HEAD_DIM)
    s = jnp.einsum('bshd,bmhd->bhsm', q, k).astype(jnp.float32) * (MEM_HEAD_DIM ** -0.5)
    p = jax.nn.softmax(s, axis=-1)
    o = jnp.einsum('bhsm,bmhd->bshd', p.astype(v.dtype), v)
    return o.reshape(B, S, MEM_WIDTH)


def setup_inputs(seed: int = 0) -> dict:
    key = jax.random.key(seed)
    ks = jax.random.split(key, 24)
    f32 = jnp.float32

    def nrm(k, shape, scale):
        return jax.random.normal(k, shape, f32) * scale

    L = DEPTH
    return {
        'x': nrm(ks[0], (BATCH, SEQ, D_MODEL), 1.0),
        'mem': nrm(ks[1], (BATCH, MEM_TOKENS, D_MODEL), 1.0),
        'w_in': nrm(ks[2], (L, D_MODEL, IN_WIDTH), D_MODEL ** -0.5),
        'hy_conv_w': nrm(ks[3], (L, SHORT_CONV, 3 * HYENA_WIDTH), SHORT_CONV ** -0.5),
        'hy_conv_b': nrm(ks[4], (L, 3 * HYENA_WIDTH), 0.02),
        'hy_w1': nrm(ks[5], (L, FILTER_EMB, FILTER_HIDDEN), FILTER_EMB ** -0.5),
        'hy_b1': nrm(ks[6], (L, FILTER_HIDDEN), 0.1),
        'hy_freq': 1.0 + nrm(ks[7], (L, FILTER_HIDDEN), 0.1),
        'hy_w2': nrm(ks[8], (L, FILTER_HIDDEN, FILTER_HIDDEN), FILTER_HIDDEN ** -0.5),
        'hy_b2': nrm(ks[9], (L, FILTER_HIDDEN), 0.1),
        'hy_w3': nrm(ks[10], (L, FILTER_HIDDEN, 2 * HYENA_ORDER * HYENA_WIDTH), FILTER_OUT_SCALE * FILTER_HIDDEN ** -0.5),
        'hy_bias': nrm(ks[11], (L, HYENA_ORDER, HYENA_WIDTH), 0.5),
        'diff_lambda': nrm(ks[12], (L, 4, DIFF_HEAD_DIM), 0.1),
        'diff_subln_g': 1.0 + nrm(ks[13], (L, 2 * DIFF_HEAD_DIM), 0.02),
        'mem_w_kv': nrm(ks[14], (L, D_MODEL, 2 * MEM_WIDTH), D_MODEL ** -0.5),
        'w_out': nrm(ks[15], (L, MIX_WIDTH, D_MODEL), DEEPNORM_BETA * MIX_WIDTH ** -0.5),
        'ln1_g': 1.0 + nrm(ks[16], (L, D_MODEL), 0.02),
        'ln1_b': nrm(ks[17], (L, D_MODEL), 0.02),
        'ffn_w_up': nrm(ks[18], (L, D_MODEL, 2 * D_FF), D_MODEL ** -0.5),
        'ffn_conv_w': nrm(ks[19], (L, SHORT_CONV, 2 * D_FF), SHORT_CONV ** -0.5),
        'ffn_conv_b': nrm(ks[20], (L, 2 * D_FF), 0.02),
        'ffn_w_down': nrm(ks[21], (L, D_FF, D_MODEL), DEEPNORM_BETA * D_FF ** -0.5),
        'ln2_g': 1.0 + nrm(ks[22], (L, D_MODEL), 0.02),
        'ln2_b': nrm(ks[23], (L, D_MODEL), 0.02),
    }


def reference(x, mem, w_in, hy_conv_w, hy_conv_b, hy_w1, hy_b1, hy_freq, hy_w2, hy_b2, hy_w3,
              hy_bias, diff_lambda, diff_subln_g, mem_w_kv, w_out, ln1_g, ln1_b,
              ffn_w_up, ffn_conv_w, ffn_conv_b, ffn_w_down, ln2_g, ln2_b):
    B, S = x.shape[0], x.shape[1]
    s1 = 3 * HYENA_WIDTH
    s2 = s1 + DIFF_WIDTH
    s3 = s2 + DIFF_WIDTH
    s4 = s3 + DIFF_WIDTH
    for l in range(DEPTH):
        lambda_init = 0.8 - 0.6 * math.exp(-0.3 * l)
        proj = x @ w_in[l]
        hy_u, dq, dk, dv, mq = jnp.split(proj, [s1, s2, s3, s4], axis=-1)
        k_f = hyena_filter_spectrum(S, hy_w1[l], hy_b1[l], hy_freq[l], hy_w2[l], hy_b2[l], hy_w3[l])
        y_h = hyena_mixer(hy_u, hy_conv_w[l], hy_conv_b[l], k_f, hy_bias[l]).astype(x.dtype)
        y_d = diff_attention(dq.reshape(B, S, DIFF_HEADS, 2, DIFF_HEAD_DIM),
                             dk.reshape(B, S, DIFF_HEADS, 2, DIFF_HEAD_DIM),
                             dv.reshape(B, S, DIFF_HEADS, 2 * DIFF_HEAD_DIM),
                             diff_lambda[l], diff_subln_g[l], lambda_init)
        y_m = memory_attention(mq, mem, mem_w_kv[l])
        mix = jnp.concatenate([y_h, y_d, y_m], axis=-1) @ w_out[l]
        x = layer_norm(DEEPNORM_ALPHA * x + mix, ln1_g[l], ln1_b[l])
        h = dwconv3(x @ ffn_w_up[l], ffn_conv_w[l], ffn_conv_b[l])
        g, u = jnp.split(h, 2, axis=-1)
        y = (jax.nn.silu(g) * u) @ ffn_w_down[l]
        x = layer_norm(DEEPNORM_ALPHA * x + y, ln2_g[l], ln2_b[l])
    return x
```

```python
import math
import contextlib
import numpy as np
import ml_dtypes
import concourse.bass as bass
import concourse.mybir as mybir
from concourse.bass_utils import run_bass_kernel_spmd

F32 = mybir.dt.float32
BF16 = mybir.dt.bfloat16
ALU = mybir.AluOpType
AF = mybir.ActivationFunctionType

NCORES = 8
NB = 2
L = 2048
D = 1024
T = NB * L
HW = 512
DFF = 2816
ALPHA = 2.0 ** 0.25
LAMBDA_INIT = 0.8 - 0.6 * math.exp(0.0)
NFFT = 4096
ENGS = ("pe", "act", "dve", "pool", "sp")
SB_BASE = 16512
SB_END = 229376


class Buf:
    __slots__ = ("name", "w", "r")

    def __init__(self, name=""):
        self.name = name
        self.w = None
        self.r = []


class Op:
    __slots__ = ("eng", "fn", "deps", "signal", "key", "count", "ord")

    def __init__(self, eng, fn, key):
        self.eng = eng
        self.fn = fn
        self.deps = {}
        self.signal = False
        self.key = key
        self.count = None
        self.ord = None


class Sched:
    def __init__(self, nc):
        self.nc = nc
        self.ops = {e: [] for e in ENGS}
        self.nord = {}
        self.allops = []
        self.last = {}
        self.fence_ops = {}

    def _dep(self, op, d, fence=False):
        if d is None or d is op:
            return
        if d.key == "pe" and op.eng == "pe":
            return
        if d.key not in ENGS and not fence:
            d = self.last[d.key]
        cur = op.deps.get(d.key)
        if cur is None or cur.ord < d.ord:
            op.deps[d.key] = d

    def fence(self):
        self.fence_ops = {k: v for k, v in self.last.items() if not str(k).startswith("cv_")}

    def add(self, eng, fn, reads=(), writes=(), dmakey=None):
        key = dmakey if dmakey is not None else eng
        op = Op(eng, fn, key)
        op.ord = self.nord.get(key, 0)
        self.nord[key] = op.ord + 1
        for d in self.fence_ops.values():
            self._dep(op, d, fence=True)
        for b in reads:
            self._dep(op, b.w)
        for b in writes:
            self._dep(op, b.w)
            for r in b.r:
                self._dep(op, r)
        for b in reads:
            b.r.append(op)
        for b in writes:
            b.w = op
            b.r = []
        self.ops[eng].append(op)
        self.allops.append(op)
        self.last[key] = op
        return op

    def emit(self, final_waits=()):
        nc = self.nc
        for op in self.allops:
            for d in op.deps.values():
                d.signal = True
        for op in final_waits:
            op.signal = True
        for op in self.allops:
            if op.key not in ENGS:
                op.signal = True
        cnt = {}
        for op in self.allops:
            if op.signal:
                inc = 1 if op.key in ENGS else 16
                cnt[op.key] = cnt.get(op.key, 0) + inc
                op.count = cnt[op.key]
        with contextlib.ExitStack() as es:
            sems = {}
            for k in cnt:
                sems[k] = es.enter_context(nc.semaphore("s_" + str(k)))
            block = es.enter_context(nc.Block())

            def run(engname):
                def body(eng):
                    waited = {}
                    for op in self.ops[engname]:
                        for k, d in op.deps.items():
                            if waited.get(k, 0) < d.count:
                                eng.wait_ge(sems[k], d.count)
                                waited[k] = d.count
                        ins = op.fn(eng)
                        if op.signal:
                            ins.then_inc(sems[op.key], 1 if op.key in ENGS else 16)
                    if engname == "sp":
                        for op in final_waits:
                            if waited.get(op.key, 0) < op.count:
                                eng.wait_ge(sems[op.key], op.count)
                                waited[op.key] = op.count
                return body

            block.tensor(run("pe"))
            block.scalar(run("act"))
            block.vector(run("dve"))
            block.gpsimd(run("pool"))
            block.sync(run("sp"))


class Arena:
    def __init__(self, nc, base, limit):
        self.nc = nc
        self.base = base
        self.off = base
        self.limit = limit
        self.n = 0

    def reset(self, to=None):
        self.off = self.base if to is None else to

    def alloc(self, shape, dtype):
        n = 1
        for s in shape[1:]:
            n *= s
        nbytes = n * (4 if dtype == F32 else 2)
        nbytes = (nbytes + 63) // 64 * 64
        off = self.off
        self.off += nbytes
        assert self.off <= self.limit, ("SBUF overflow", self.off, self.limit)
        self.n += 1
        return self.nc.alloc_sbuf_tensor_at("a%d" % self.n, list(shape), dtype, offset=off)


def sl(i, n):
    return slice(i * n, (i + 1) * n)


def build(dbg=False, stop_after=None):
    nc = bass.Bass("TRN2", target_bir_lowering=False)
    S = Sched(nc)
    skind = "ExternalOutput" if dbg else "Internal"

    def din(name, shape, dt=F32):
        return nc.dram_tensor(name, list(shape), dt, kind="ExternalInput").ap()

    def dscr(name, shape, dt):
        return nc.dram_tensor(name, list(shape), dt, kind=skind).ap()

    x = din("x", [NB, L, D])
    mem = din("mem", [NB, 256, D])
    w_in = din("w_in", [D, 3584])
    w_kv = din("w_kv", [D, 1024])
    w_out = din("w_out", [1536, D])
    w_up = din("w_up", [D, 2 * DFF])
    w_down = din("w_down", [DFF, D])
    hcw_d = din("hcw", [128, 12, 3])
    hcb_d = din("hcb", [128, 12])
    fcw_d = din("fcw", [128, 44, 3])
    fcb_d = din("fcb", [128, 44])
    w1_d = din("w1", [33, 64])
    w2_d = din("w2", [64, 64])
    w3_d = din("w3", [64, 2048])
    fv_d = din("fv", [64, 3])
    hb_d = din("hbias", [128, 2, 512])
    lam_d = din("lam", [128, 256])
    subg_d = din("subg", [128, 1])
    ln_d = din("lnp", [128, 4, 1024])
    identf_d = din("identf", [128, 128])
    identb_d = din("identb", [128, 128], BF16)
    prot_d = din("prot", [128, 128])
    ropec_d = din("ropec", [128, L])
    ropes_d = din("ropes", [128, L])
    zT_d = din("zT", [33, L])
    dec_d = din("decay", [128, 16, 512])
    gc_d = din("gc", [16, 128, 16, 128], BF16)
    gsf_d = din("gsf", [16, 128, 16, 128], BF16)
    gsi_d = din("gsi", [16, 128, 16, 128], BF16)
    out = nc.dram_tensor("out", [NB, L, D], F32, kind="ExternalOutput").ap()

    win_b = dscr("win_b", [7, 128, 8, 512], BF16)
    wkv_b = dscr("wkv_b", [128, 8, 1024], BF16)
    wout_b = dscr("wout_b", [128, 12, 1024], BF16)
    wup_b = dscr("wup_b", [22, 128, 8, 2, 128], BF16)
    wdn_b = dscr("wdn_b", [128, 22, 1024], BF16)
    kf_s = dscr("kf_s", [16, 128, 2, 2, 512], BF16)
    xT_s = dscr("xT_s", [NB, 128, 8, L], BF16)
    y_s = dscr("y_s", [1536, T], BF16)
    x1_s = dscr("x1_s", [T, D], F32)
    x1T_s = dscr("x1T_s", [128, 8, T], BF16)

    pbig = nc.alloc_psum_tensor("pbig", [128, 8, 512], F32)
    pb = [pbig[:, i, :] for i in range(8)]
    PB = [Buf("pb%d" % i) for i in range(8)]

    per = Arena(nc, SB_BASE, SB_BASE + 6144)
    identf = per.alloc([128, 128], F32)
    identb = per.alloc([128, 128], BF16)
    prot = per.alloc([128, 128], F32)
    onesb = per.alloc([128, 128], BF16)
    onesf = per.alloc([128, 128], F32)
    hcw = per.alloc([128, 12, 3], F32)
    hcb = per.alloc([128, 12], F32)
    fcw = per.alloc([128, 44, 3], F32)
    fcb = per.alloc([128, 44], F32)
    subg = per.alloc([128, 1], F32)
    gsc = per.alloc([128, 1], F32)
    neglam = per.alloc([128, 1], F32)
    epsr = per.alloc([128, 1], F32)
    lamt = per.alloc([128, 8], F32)
    lamp = per.alloc([128, 256], F32)
    A = Arena(nc, per.off, SB_END)
    BC = Buf("consts")

    def cload(dst, src):
        S.add("sp", lambda e: e.dma_start(out=dst, in_=src), writes=[BC], dmakey="c0")

    cload(identf[:], identf_d)
    cload(identb[:], identb_d)
    cload(prot[:], prot_d)
    cload(hcw[:], hcw_d)
    cload(hcb[:], hcb_d)
    cload(fcw[:], fcw_d)
    cload(fcb[:], fcb_d)
    cload(subg[:], subg_d)
    cload(lamp[:], lam_d)
    BC2 = Buf("consts2")
    S.add("pool", lambda e: e.memset(onesb[:], 1.0), writes=[BC2])
    S.add("pool", lambda e: e.memset(onesf[:], 1.0), writes=[BC2])
    S.add("pool", lambda e: e.memset(epsr[:], 1e-5), writes=[BC2])
    S.add("dve", lambda e: e.tensor_tensor(lamp[:, 0:64], lamp[:, 0:64], lamp[:, 64:128], ALU.mult), reads=[BC], writes=[BC2])
    S.add("dve", lambda e: e.tensor_tensor(lamp[:, 128:192], lamp[:, 128:192], lamp[:, 192:256], ALU.mult), reads=[BC2], writes=[BC2])
    S.add("dve", lambda e: e.reduce_sum(lamt[:, 0:1], lamp[:, 0:64], mybir.AxisListType.X), reads=[BC2], writes=[BC2])
    S.add("dve", lambda e: e.reduce_sum(lamt[:, 1:2], lamp[:, 128:192], mybir.AxisListType.X), reads=[BC2], writes=[BC2])
    S.add("act", lambda e: e.activation(lamt[:, 2:4], lamt[:, 0:2], AF.Exp), reads=[BC2], writes=[BC2])
    S.add("dve", lambda e: e.scalar_tensor_tensor(neglam[:], lamt[:, 3:4], -LAMBDA_INIT, lamt[:, 2:3], ALU.add, ALU.subtract), reads=[BC2], writes=[BC2])
    S.add("dve", lambda e: e.tensor_scalar_mul(gsc[:], subg[:], 1.0 - LAMBDA_INIT), reads=[BC, BC2], writes=[BC2])
    CONST = [BC, BC2]

    BW = {k: Buf(k) for k in ("in", "kv", "out", "up", "down")}

    def phase_W():
      for k in range(8):
        S.add("pool", lambda e, k=k: e.dma_start(out=win_b.rearrange("g p k c -> p g k c")[:, :, k, :],
                                                  in_=w_in[sl(k, 128), :].rearrange("p (g c) -> p g c", g=7)),
              writes=[BW["in"]], dmakey="cv_in")
      for k in range(8):
        S.add("pool", lambda e, k=k: e.dma_start(out=wkv_b[:, k, :], in_=w_kv[sl(k, 128), :]), writes=[BW["kv"]], dmakey="cv_kv")
      for k in range(12):
        S.add("pool", lambda e, k=k: e.dma_start(out=wout_b[:, k, :], in_=w_out[sl(k, 128), :]), writes=[BW["out"]], dmakey="cv_out")
      for k in range(8):
        for g in range(2):
            S.add("pool", lambda e, k=k, g=g: e.dma_start(
                out=wup_b.rearrange("m p k g c -> p m k g c")[:, :, k, g, :],
                in_=w_up[sl(k, 128), g * DFF:(g + 1) * DFF].rearrange("p (m c) -> p m c", m=22)),
                writes=[BW["up"]], dmakey="cv_up")
      for k in range(22):
        S.add("pool", lambda e, k=k: e.dma_start(out=wdn_b[:, k, :], in_=w_down[sl(k, 128), :]), writes=[BW["down"]], dmakey="cv_down")

    finals = []
    BKF = Buf("kf_s")

    def phase_A():
        A.reset()
        zT = A.alloc([33, L], F32)
        w1 = A.alloc([33, 64], F32)
        w2 = A.alloc([64, 64], F32)
        w3 = A.alloc([64, 2048], F32)
        w3n = A.alloc([64, 1024], BF16)
        w3h = A.alloc([64, 2048], BF16)
        h2h = A.alloc([64, L], BF16)
        fv = A.alloc([64, 3], F32)
        fb = A.alloc([64, 2], F32)
        hT = [A.alloc([64, L], F32) for _ in range(2)]
        h2b = A.alloc([64, L], BF16)
        arg = [A.alloc([64, 512], F32) for _ in range(2)]
        kk = [A.alloc([64, 512], F32) for _ in range(2)]
        dec = A.alloc([128, 16, 512], F32)
        hsd = A.alloc([128, 16, 2, 2, 512], BF16)
        gcb = [A.alloc([128, 16, 128], BF16) for _ in range(2)]
        gsb = [A.alloc([128, 16, 128], BF16) for _ in range(2)]
        kst = [A.alloc([128, 2, 2, 512], BF16) for _ in range(2)]
        BA = Buf("Aconst")
        for dst, src in ((zT[:], zT_d), (w1[:], w1_d), (w2[:], w2_d), (w3[:], w3_d), (fv[:], fv_d), (dec[:], dec_d)):
            S.add("sp", lambda e, dst=dst, src=src: e.dma_start(out=dst, in_=src), writes=[BA], dmakey="c1A")
        Bfb = Buf("fb")
        S.add("dve", lambda e: e.tensor_tensor(fb[:, 0:1], fv[:, 0:1], fv[:, 2:3], ALU.mult), reads=[BA], writes=[Bfb])
        S.add("dve", lambda e: e.tensor_tensor(fb[:, 1:2], fv[:, 1:2], fv[:, 2:3], ALU.mult), reads=[BA, Bfb], writes=[Bfb])
        Bw3n = Buf("w3n")
        for o in range(2):
            S.add("pool", lambda e, o=o: e.tensor_scalar_mul(w3n[:, sl(o, 512)], w3[:, o * 1024 + 512:(o + 1) * 1024], -1.0),
                  reads=[BA, Bw3n], writes=[Bw3n])
        S.add("pool", lambda e: e.tensor_copy(w3h[:], w3[:]), reads=[BA, Bw3n], writes=[Bw3n])
        BhT = [[Buf("hT%d_%d" % (l, c)) for c in range(4)] for l in range(2)]
        Barg = [Buf("arg0"), Buf("arg1")]
        Bkk = [Buf("kk0"), Buf("kk1")]
        MAG = 12582912.0
        for layer in range(2):
            lw = w1 if layer == 0 else w2
            for c in range(4):
                rhs = zT[:, sl(c, 512)] if layer == 0 else hT[0][:, sl(c, 512)]
                rb = [BA] if layer == 0 else [BhT[0][c]]
                S.add("pe", lambda e, lw=lw, rhs=rhs, c=c: e.matmul(pb[c][0:64, :], lw[:], rhs, start=True, stop=True),
                      reads=[BA] + rb, writes=[PB[c]])
                s = c % 2
                S.add("dve", lambda e, c=c, s=s, layer=layer: e.tensor_scalar(arg[s][:], pb[c][0:64, :], fv[:, 2:3], fb[:, layer:layer + 1], ALU.mult, ALU.add),
                      reads=[PB[c], BA, Bfb], writes=[Barg[s]])
                S.add("dve", lambda e, s=s: e.tensor_scalar(kk[s][:], arg[s][:], 1.0 / (2 * math.pi), MAG, ALU.mult, ALU.add),
                      reads=[Barg[s]], writes=[Bkk[s]])
                S.add("dve", lambda e, s=s: e.tensor_scalar(kk[s][:], kk[s][:], -MAG, -2 * math.pi, ALU.add, ALU.mult),
                      reads=[Bkk[s]], writes=[Bkk[s]])
                S.add("dve", lambda e, s=s: e.tensor_tensor(arg[s][:], arg[s][:], kk[s][:], ALU.add),
                      reads=[Barg[s], Bkk[s]], writes=[Barg[s]])
                S.add("act", lambda e, s=s, c=c, layer=layer: e.activation(hT[layer][:, sl(c, 512)], arg[s][:], AF.Sin, scale=0.999999),
                      reads=[Barg[s]], writes=[BhT[layer][c]])
        Bh2b = Buf("h2b")
        S.add("pool", lambda e: e.tensor_copy(h2b[:], hT[1][:]), reads=BhT[1], writes=[Bh2b])
        S.add("pool", lambda e: e.memset(h2b[:, 0:1], 0.0), reads=[Bh2b], writes=[Bh2b])
        S.add("dve", lambda e: e.tensor_copy(h2h[:], hT[1][:]), reads=BhT[1], writes=[Bh2b])
        Bhsd = [Buf("hsd%d" % j) for j in range(16)]
        ib = 0
        for j in range(16):
            for o in range(2):
                for part in range(2):
                    bank = 4 + (ib % 4)
                    ib += 1
                    wb_ = w3h[:, o * 1024 + 512:(o + 1) * 1024] if part == 0 else w3n[:, sl(o, 512)]

                    def mmf(e, bank=bank, j=j, o=o, wb_=wb_):
                        e.matmul(pb[bank][:], h2h[:, sl(j, 128)], w3h[:, o * 1024:o * 1024 + 512], start=True, stop=False)
                        return e.matmul(pb[bank][:], h2b[:, sl(j, 128)], wb_, start=False, stop=True)
                    S.add("pe", mmf, reads=[BA, Bw3n, Bh2b] + BhT[1], writes=[PB[bank]])
                    S.add("dve", lambda e, bank=bank, j=j, o=o, part=part: e.tensor_tensor(hsd[:, j, part, o, :], pb[bank][:], dec[:, j, :], ALU.mult),
                          reads=[PB[bank], BA], writes=[Bhsd[j]])
        Bg = [Buf("gA0"), Buf("gA1")]
        Bg2 = [Buf("gB0"), Buf("gB1")]
        Bkst = [Buf("kst0"), Buf("kst1")]

        def gload(m):
            s = m % 2
            S.add("sp", lambda e: e.dma_start(out=gcb[s][:], in_=gc_d[m]), writes=[Bg[s]], dmakey="g%d" % s)
            S.add("sp", lambda e: e.dma_start(out=gsb[s][:], in_=gsf_d[m]), writes=[Bg2[s]], dmakey="gs%d" % s)
        gload(0)
        ib = 0
        for m in range(16):
            if m + 1 < 16:
                gload(m + 1)
            s = m % 2
            for part in range(2):
                g = gcb[s] if part == 0 else gsb[s]
                for o in range(2):
                    bank = ib % 4
                    ib += 1

                    def mms(e, bank=bank, g=g, part=part, o=o):
                        for k in range(16):
                            ins = e.matmul(pb[bank][:], g[:, k, :], hsd[:, k, part, o, :], start=(k == 0), stop=(k == 15))
                        return ins
                    S.add("pe", mms, reads=[Bg[s], Bg2[s]] + Bhsd, writes=[PB[bank]])
                    S.add("act", lambda e, bank=bank, s=s, o=o, part=part: e.activation(kst[s][:, o, part, :], pb[bank][:], AF.Copy),
                          reads=[PB[bank]], writes=[Bkst[s]])
            if m == 0:
                for o in range(2):
                    bank = ib % 4
                    ib += 1

                    def mmn(e, bank=bank, o=o, s=s):
                        for k in range(16):
                            ins = e.matmul(pb[bank][0:1, :], gsb[s][:, k, 0:1], hsd[:, k, 0, o, :], start=(k == 0), stop=(k == 15))
                        return ins
                    S.add("pe", mmn, reads=[Bg[s], Bg2[s]] + Bhsd, writes=[PB[bank]])
                    S.add("act", lambda e, bank=bank, s=s, o=o: e.activation(kst[s][0:1, o, 1, :], pb[bank][0:1, :], AF.Copy),
                          reads=[PB[bank]], writes=[Bkst[s]])
            S.add("sp", lambda e, m=m, s=s: e.dma_start(out=kf_s[m], in_=kst[s][:]), reads=[Bkst[s]], writes=[BKF], dmakey="kst%d" % s)

    BXT = Buf("xT_s")
    BY = Buf("y_s")

    def make_xT(b, xT, BxT, xst, Bxst, src, ntile, key):
        def xload(j):
            s = j % 2
            S.add("sp", lambda e: e.dma_start(out=xst[s][:], in_=src[b, sl(j, 128), :]), writes=[Bxst[s]], dmakey="%s%d" % (key, s))
        xload(0)
        for j in range(ntile):
            if j + 1 < ntile:
                xload(j + 1)
            s = j % 2
            for h in range(2):
                bank = 6 + h

                def tr(e, s=s, h=h, bank=bank):
                    for q in range(4):
                        ins = e.transpose(pb[bank][:, sl(q, 128)], xst[s][:, sl(h * 4 + q, 128)], identf[:])
                    return ins
                S.add("pe", tr, reads=[Bxst[s]] + CONST, writes=[PB[bank]])
                eng = "act" if h == 0 else "dve"
                if eng == "act":
                    S.add("act", lambda e, h=h, j=j, bank=bank: e.activation(xT[:, h * 4:(h + 1) * 4, sl(j, 128)], pb[bank][:].rearrange("p (q t) -> p q t", q=4), AF.Copy),
                          reads=[PB[bank]], writes=[BxT[j][h]])
                else:
                    S.add("dve", lambda e, h=h, j=j, bank=bank: e.tensor_copy(xT[:, h * 4:(h + 1) * 4, sl(j, 128)], pb[bank][:].rearrange("p (q t) -> p q t", q=4)),
                          reads=[PB[bank]], writes=[BxT[j][h]])

    def phase_H():
        S.fence()
        A.reset()
        tm = [A.alloc([128, 16, 1024], BF16) for _ in range(3)]
        mark = A.off
        xst = [A.alloc([128, D], F32) for _ in range(2)]
        xT = A.alloc([128, 8, L], BF16)
        wbf = [A.alloc([128, 8, 512], BF16) for _ in range(2)]
        upad = [A.alloc([128, L + 2], F32) for _ in range(2)]
        tmpc = [A.alloc([128, L], F32) for _ in range(2)]
        ubf = [A.alloc([128, L], BF16) for _ in range(2)]
        Bxst = [Buf("xst0"), Buf("xst1")]
        Bwbf = [Buf("wbf0"), Buf("wbf1")]
        Bupad = [Buf("upad0"), Buf("upad1")]
        Btmpc = [Buf("tmpc0"), Buf("tmpc1")]
        Bubf = [Buf("ubf0"), Buf("ubf1")]
        Btm = [[[[Buf("tm") for _ in range(4)] for _ in range(4)] for _ in range(NB)] for _ in range(3)]
        for s in range(2):
            S.add("dve", lambda e, s=s: e.memset(upad[s][:, 0:1], 0.0), writes=[Bupad[s]])
            S.add("dve", lambda e, s=s: e.memset(upad[s][:, L + 1:L + 2], 0.0), writes=[Bupad[s]])
        it = 0
        pendB = [None]
        BxT = [[Buf("xT"), Buf("xT")] for _ in range(16)]
        for b in range(NB):
            make_xT(b, xT, BxT, xst, Bxst, x, 16, "xst")
            allxT = [bb for pr in BxT for bb in pr]
            S.add("pool", lambda e, b=b: e.dma_start(out=xT_s[b], in_=xT[:]), reads=allxT, writes=[BXT], dmakey="xTst")
            for g in range(3):
                ws = (b * 3 + g) % 2
                S.add("sp", lambda e, g=g, ws=ws: e.dma_start(out=wbf[ws][:], in_=win_b[g]), reads=[BW["in"]], writes=[Bwbf[ws]], dmakey="w%d" % ws)
                for ci in range(4):
                    mt = g * 4 + ci
                    us = it % 2
                    it += 1
                    for c in range(4):
                        bank = c

                        def mm(e, ws=ws, ci=ci, c=c, bank=bank):
                            for k in range(8):
                                ins = e.matmul(pb[bank][:], wbf[ws][:, k, sl(ci, 128)], xT[:, k, sl(c, 512)], start=(k == 0), stop=(k == 7))
                            return ins
                        S.add("pe", mm, reads=[Bwbf[ws]] + [bb for j in range(4 * c, 4 * c + 4) for bb in BxT[j]], writes=[PB[bank]])
                        S.add("act", lambda e, us=us, c=c, bank=bank: e.activation(upad[us][:, 1 + c * 512:1 + (c + 1) * 512], pb[bank][:], AF.Copy),
                              reads=[PB[bank]], writes=[Bupad[us]])
                    S.add("act", lambda e, us=us, mt=mt: e.activation(tmpc[us][:], upad[us][:, 1:L + 1], AF.Identity, bias=hcb[:, mt:mt + 1], scale=hcw[:, mt, 1:2]),
                          reads=[Bupad[us]] + CONST, writes=[Btmpc[us]])
                    S.add("dve", lambda e, us=us, mt=mt: e.scalar_tensor_tensor(tmpc[us][:], upad[us][:, 0:L], hcw[:, mt, 0:1], tmpc[us][:], ALU.mult, ALU.add),
                          reads=[Bupad[us], Btmpc[us]] + CONST, writes=[Btmpc[us]])
                    S.add("dve", lambda e, us=us, mt=mt: e.scalar_tensor_tensor(ubf[us][:], upad[us][:, 2:L + 2], hcw[:, mt, 2:3], tmpc[us][:], ALU.mult, ALU.add),
                          reads=[Bupad[us], Btmpc[us]] + CONST, writes=[Bubf[us]])
                    def stageB(us=us, g=g, b=b, ci=ci):
                        for jg in range(4):
                            bank = 4 + (jg % 4)
                            pbb = pb[bank][:].bitcast(BF16)

                            def tr(e, us=us, jg=jg, pbb=pbb):
                                for q in range(4):
                                    ins = e.transpose(pbb[:, sl(q, 128)], ubf[us][:, sl(jg * 4 + q, 128)], identb[:])
                                return ins
                            S.add("pe", tr, reads=[Bubf[us]] + CONST, writes=[PB[bank]])
                            dst = tm[g][:, jg * 4:(jg + 1) * 4, b * 512 + ci * 128:b * 512 + (ci + 1) * 128]
                            src_ = pbb[:, 0:512].rearrange("p (q t) -> p q t", q=4)
                            if jg % 2 == 0:
                                S.add("act", lambda e, dst=dst, src_=src_: e.activation(dst, src_, AF.Copy), reads=[PB[bank]], writes=[Btm[g][b][jg][ci]])
                            else:
                                S.add("dve", lambda e, dst=dst, src_=src_: e.tensor_copy(dst, src_), reads=[PB[bank]], writes=[Btm[g][b][jg][ci]])
                    if pendB[0] is not None:
                        pendB[0]()
                    pendB[0] = stageB
        pendB[0]()
        if stop_after == "H1":
            return tm
        S.fence()
        A.reset(mark)
        Y = A.alloc([128, 16, 2, 1024], BF16)
        gcb = [A.alloc([128, 16, 128], BF16) for _ in range(2)]
        gsb = [A.alloc([128, 16, 128], BF16) for _ in range(2)]
        kl = [A.alloc([128, 2, 512], BF16) for _ in range(2)]
        zs = [A.alloc([128, 512], F32) for _ in range(2)]
        tt_ = [A.alloc([128, 512], F32) for _ in range(4)]
        tinv = [A.alloc([128, 512], F32) for _ in range(2)]
        hb = A.alloc([128, 2, 512], F32)
        ystg = [tt_[0][:].bitcast(BF16), tt_[1][:].bitcast(BF16)]
        Bhb = Buf("hb")
        S.add("sp", lambda e: e.dma_start(out=hb[:], in_=hb_d), writes=[Bhb], dmakey="c1H")
        Bg = [Buf("g0"), Buf("g1")]
        Bg2 = [Buf("gs0"), Buf("gs1")]
        Bkl = [Buf("kl0"), Buf("kl1")]
        Bzs = [Buf("zr"), Buf("zi")]
        Btt = [Buf("t%d" % i) for i in range(4)]
        Btinv = [Buf("tinv0"), Buf("tinv1")]
        Btm2 = [[[Buf("tm2") for _ in range(16)] for _ in range(NB)] for _ in range(3)]
        BYb = [[Buf("Y") for _ in range(NB)] for _ in range(16)]
        for o in range(2):
            zin = tm[0] if o == 0 else tm[1]
            zi_i = 0 if o == 0 else 1
            xg = tm[1] if o == 0 else tm[2]
            xg_i = 1 if o == 0 else 2

            def gload(m, inv):
                s = m % 2
                S.add("sp", lambda e: e.dma_start(out=gcb[s][:], in_=gc_d[m]), writes=[Bg[s]], dmakey="g%d" % s)
                S.add("sp", lambda e: e.dma_start(out=gsb[s][:], in_=(gsi_d if inv else gsf_d)[m]), writes=[Bg2[s]], dmakey="gs%d" % s)
                if not inv:
                    S.add("sp", lambda e, o=o: e.dma_start(out=kl[s][:], in_=kf_s[m, :, o, :, :]), reads=[BKF], writes=[Bkl[s]], dmakey="kl%d" % s)
            gload(0, False)
            for m in range(16):
                if m + 1 < 16:
                    gload(m + 1, False)
                s = m % 2
                for n in range(NB):
                    for part in range(2):
                        g = gcb[s] if part == 0 else gsb[s]
                        bank = part + 2 * ((m * NB + n) % 2)

                        def mmf(e, g=g, n=n, bank=bank, zin=zin):
                            for k in range(16):
                                ins = e.matmul(pb[bank][:], g[:, k, :], zin[:, k, sl(n, 512)], start=(k == 0), stop=(k == 15))
                            return ins
                        S.add("pe", mmf, reads=[Bg[s], Bg2[s]] + Btm2[zi_i][n], writes=[PB[bank]])
                        S.add("act", lambda e, part=part, bank=bank: e.activation(zs[part][:], pb[bank][:], AF.Copy), reads=[PB[bank]], writes=[Bzs[part]])
                    kr = kl[s][:, 0, :]
                    ki = kl[s][:, 1, :]
                    S.add("dve", lambda e, kr=kr: e.tensor_tensor(tt_[0][:], zs[0][:], kr, ALU.mult), reads=[Bzs[0], Bkl[s]], writes=[Btt[0]])
                    S.add("pool", lambda e, ki=ki: e.tensor_tensor(tt_[1][:], zs[1][:], ki, ALU.mult), reads=[Bzs[1], Bkl[s]], writes=[Btt[1]])
                    S.add("pool", lambda e, ki=ki: e.tensor_tensor(tt_[2][:], zs[0][:], ki, ALU.mult), reads=[Bzs[0], Bkl[s]], writes=[Btt[2]])
                    S.add("dve", lambda e, kr=kr: e.tensor_tensor(tt_[3][:], zs[1][:], kr, ALU.mult), reads=[Bzs[1], Bkl[s]], writes=[Btt[3]])
                    S.add("dve", lambda e, m=m, n=n: e.tensor_tensor(Y[:, m, 0, sl(n, 512)], tt_[0][:], tt_[1][:], ALU.subtract),
                          reads=[Btt[0], Btt[1]], writes=[BYb[m][n]])
                    S.add("dve", lambda e, m=m, n=n: e.tensor_tensor(Y[:, m, 1, sl(n, 512)], tt_[2][:], tt_[3][:], ALU.add),
                          reads=[Btt[2], Btt[3]], writes=[BYb[m][n]])
                    if m == 0:
                        S.add("dve", lambda e, n=n, kr=kr: e.scalar_tensor_tensor(Y[0:1, 0, 0, sl(n, 512)], zs[0][0:1, :], 0.5, kr[0:1, :], ALU.mult, ALU.mult),
                              reads=[Bzs[0], Bkl[s], BYb[m][n]], writes=[BYb[m][n]])
                        S.add("dve", lambda e, n=n, ki=ki: e.scalar_tensor_tensor(Y[0:1, 0, 1, sl(n, 512)], zs[1][0:1, :], 0.5, ki[0:1, :], ALU.mult, ALU.mult),
                              reads=[Bzs[1], Bkl[s], BYb[m][n]], writes=[BYb[m][n]])
            gload(0, True)
            for j in range(16):
                if j + 1 < 16:
                    gload(j + 1, True)
                s = j % 2
                for n in range(NB):
                    bank = 4 + (j * NB + n) % 4

                    def mmi(e, s=s, n=n, bank=bank):
                        for k in range(16):
                            e.matmul(pb[bank][:], gcb[s][:, k, :], Y[:, k, 0, sl(n, 512)], start=(k == 0), stop=False)
                            ins = e.matmul(pb[bank][:], gsb[s][:, k, :], Y[:, k, 1, sl(n, 512)], start=False, stop=(k == 15))
                        return ins
                    S.add("pe", mmi, reads=[Bg[s], Bg2[s]] + [BYb[k][n] for k in range(16)], writes=[PB[bank]])
                    ts_ = (j * NB + n) % 2
                    S.add("pool", lambda e, j=j, n=n, ts_=ts_, zin=zin, o=o: e.tensor_tensor(tinv[ts_][:], zin[:, j, sl(n, 512)], hb[:, o, :], ALU.mult),
                          reads=[Btm2[zi_i][n][j], Bhb], writes=[Btinv[ts_]])
                    S.add("dve", lambda e, bank=bank, ts_=ts_: e.scalar_tensor_tensor(tinv[ts_][:], pb[bank][:], 2.0 / NFFT, tinv[ts_][:], ALU.mult, ALU.add),
                          reads=[PB[bank], Btinv[ts_]], writes=[Btinv[ts_]])
                    S.add("dve", lambda e, j=j, n=n, ts_=ts_, xg=xg: e.tensor_tensor(xg[:, j, sl(n, 512)], tinv[ts_][:], xg[:, j, sl(n, 512)], ALU.mult),
                          reads=[Btinv[ts_], Btm2[xg_i][n][j]], writes=[Btm2[xg_i][n][j]])
        Bys = [Buf("ystg0"), Buf("ystg1")]
        iy = 0
        for b in range(NB):
            for ci in range(4):
                for half in range(2):
                    s = iy % 2
                    iy += 1
                    bank = s
                    pbb = pb[bank][:].bitcast(BF16)

                    def tr(e, b=b, ci=ci, half=half, pbb=pbb):
                        for q in range(8):
                            j = half * 8 + q
                            ins = e.transpose(pbb[:, sl(q, 128)], tm[2][:, j, b * 512 + ci * 128:b * 512 + (ci + 1) * 128], identb[:])
                        return ins
                    S.add("pe", tr, reads=[Btm2[2][b][jj] for jj in range(half * 8, half * 8 + 8)] + CONST, writes=[PB[bank]])
                    S.add("act", lambda e, s=s, pbb=pbb: e.activation(ystg[s], pbb, AF.Copy), reads=[PB[bank]] + Btt, writes=[Bys[s], Btt[s]])
                    S.add("pool", lambda e, s=s, b=b, ci=ci, half=half: e.dma_start(out=y_s[sl(ci, 128), b * L + half * 1024:b * L + (half + 1) * 1024], in_=ystg[s]),
                          reads=[Bys[s]], writes=[BY], dmakey="yst%d" % s)
        return tm

    def phase_D():
        S.fence()
        A.reset()
        xT = A.alloc([128, 8, L], BF16)
        ropec = A.alloc([128, L], F32)
        ropes = A.alloc([128, L], F32)
        qk = A.alloc([128, 12, L], BF16)
        Vt = A.alloc([128, 16, 512], BF16)
        wbf = [A.alloc([128, 8, 512], BF16) for _ in range(2)]
        mst = [A.alloc([128, D], F32) for _ in range(2)]
        memT = A.alloc([128, 8, 256], BF16)
        kmT = A.alloc([128, 4, 256], BF16)
        vm = A.alloc([128, 2, 512], BF16)
        mqT = [A.alloc([128, 512], BF16) for _ in range(2)]
        E = [A.alloc([128, 1024], BF16) for _ in range(4)]
        qf = [A.alloc([128, 512], F32) for _ in range(2)]
        r1_ = [A.alloc([128, 512], F32) for _ in range(2)]
        r2_ = [A.alloc([128, 512], F32) for _ in range(2)]
        epr = A.alloc([128, 1024], F32)
        ep = [epr[:, 0:512], epr[:, 512:1024]] + [A.alloc([128, 512], F32) for _ in range(5)]
        ystg = [A.alloc([128, L], BF16) for _ in range(2)]
        Brope = Buf("rope")
        Bwbf = [Buf("wbf0"), Buf("wbf1")]
        Bmst = [Buf("mst0"), Buf("mst1")]
        BmqT = [Buf("mq0"), Buf("mq1")]
        BE = [Buf("E%d" % i) for i in range(4)]
        Bqf = [Buf("qf0"), Buf("qf1")]
        Br1 = [Buf("r10"), Buf("r11")]
        Br2 = [Buf("r20"), Buf("r21")]
        Bep = [Buf("ep%d" % i) for i in range(7)]
        Bys = [Buf("ys0"), Buf("ys1")]
        wi = [0]
        iy = [0]

        def wload(src, rd):
            ws = wi[0] % 2
            wi[0] += 1
            S.add("sp", lambda e: e.dma_start(out=wbf[ws][:], in_=src), reads=[rd], writes=[Bwbf[ws]], dmakey="w%d" % ws)
            return ws

        BxT = Buf("xTd")
        BmT = [[Buf("mT"), Buf("mT")] for _ in range(2)]
        BkmT = Buf("kmT")
        Bvm = Buf("vm")
        BVt = [Buf("Vt") for _ in range(16)]
        Bqk = [[Buf("qk") for _ in range(4)] for _ in range(12)]
        S.add("pool", lambda e: e.memset(qk[64:128, 0:4, :], 0.0), writes=[bb for t_ in range(0, 4) for bb in Bqk[t_]])
        S.add("pool", lambda e: e.memset(qk[0:64, 4:8, :], 0.0), writes=[bb for t_ in range(4, 8) for bb in Bqk[t_]])
        for b in range(NB):
            S.add("sp", lambda e, b=b: e.dma_start(out=xT[:], in_=xT_s[b]), reads=[BXT], writes=[BxT], dmakey="xTld")
            make_xT(b, memT, BmT, mst, Bmst, mem, 2, "mst")
            allmT = [bb for pr in BmT for bb in pr]
            ws = wload(wkv_b[:, :, 0:512], BW["kv"])
            for h in range(4):
                bank = h % 2

                def mmk(e, ws=ws, h=h, bank=bank):
                    for k in range(8):
                        ins = e.matmul(pb[bank][:, 0:256], wbf[ws][:, k, sl(h, 128)], memT[:, k, :], start=(k == 0), stop=(k == 7))
                    return ins
                S.add("pe", mmk, reads=[Bwbf[ws]] + allmT, writes=[PB[bank]])
                S.add("act", lambda e, h=h, bank=bank: e.activation(kmT[:, h, :], pb[bank][:, 0:256], AF.Copy), reads=[PB[bank]], writes=[BkmT])
            ws = wload(wkv_b[:, :, 512:1024], BW["kv"])
            for mt in range(2):
                bank = 2 + mt

                def mmv(e, ws=ws, mt=mt, bank=bank):
                    for k in range(8):
                        ins = e.matmul(pb[bank][:], memT[:, k, sl(mt, 128)], wbf[ws][:, k, :], start=(k == 0), stop=(k == 7))
                    return ins
                S.add("pe", mmv, reads=[Bwbf[ws]] + allmT, writes=[PB[bank]])
                S.add("act", lambda e, mt=mt, bank=bank: e.activation(vm[:, mt, :], pb[bank][:], AF.Copy), reads=[PB[bank]], writes=[Bvm])
            ws = wload(win_b[6], BW["in"])
            items = [(h, c) for h in range(4) for c in range(4)]
            ysl = {}
            for h in range(4):
                ysl[h] = iy[0] % 2
                iy[0] += 1

            def stA(i, ws=ws):
                h, c = items[i]
                ms = i % 2
                bank = 0 if i % 2 == 0 else 5

                def mm(e, ws=ws, h=h, c=c, bank=bank):
                    for k in range(8):
                        ins = e.matmul(pb[bank][:], wbf[ws][:, k, sl(h, 128)], xT[:, k, sl(c, 512)], start=(k == 0), stop=(k == 7))
                    return ins
                S.add("pe", mm, reads=[Bwbf[ws], BxT], writes=[PB[bank]])
                S.add("dve", lambda e, ms=ms, bank=bank: e.tensor_copy(mqT[ms][:], pb[bank][:]), reads=[PB[bank]], writes=[BmqT[ms]])

            def stB(i):
                h, c = items[i]
                ms = i % 2
                for mt in range(2):
                    bank = (1 + mt) if i % 2 == 0 else (6 + mt)
                    es = 2 * (i % 2) + mt
                    S.add("pe", lambda e, h=h, mt=mt, ms=ms, bank=bank: e.matmul(pb[bank][:], kmT[:, h, sl(mt, 128)], mqT[ms][:], start=True, stop=True),
                          reads=[BkmT, BmqT[ms]], writes=[PB[bank]])
                    S.add("act", lambda e, es=es, bank=bank: e.activation(E[es][:, 0:512], pb[bank][:], AF.Exp, scale=128.0 ** -0.5),
                          reads=[PB[bank]], writes=[BE[es]])

            def stC(i):
                h, c = items[i]
                ys = ysl[h]
                e0 = 2 * (i % 2)

                def mmo(e, h=h, e0=e0):
                    e.matmul(pb[3][:], vm[:, 0, sl(h, 128)], E[e0][:, 0:512], start=True, stop=False)
                    e.matmul(pb[3][:], vm[:, 1, sl(h, 128)], E[e0 + 1][:, 0:512], start=False, stop=True)
                    e.matmul(pb[4][:], onesb[:], E[e0][:, 0:512], start=True, stop=False)
                    return e.matmul(pb[4][:], onesb[:], E[e0 + 1][:, 0:512], start=False, stop=True)
                S.add("pe", mmo, reads=[Bvm, BE[e0], BE[e0 + 1]] + CONST, writes=[PB[3], PB[4]])
                S.add("act", lambda e: e.activation(ep[0][:], pb[4][:], AF.Ln), reads=[PB[4]], writes=[Bep[0]])
                S.add("act", lambda e: e.activation(ep[0][:], ep[0][:], AF.Exp, scale=-1.0), reads=[Bep[0]], writes=[Bep[0]])
                S.add("dve", lambda e, ys=ys, c=c: e.tensor_tensor(ystg[ys][:, sl(c, 512)], pb[3][:], ep[0][:], ALU.mult),
                      reads=[PB[3], Bep[0]], writes=[Bys[ys]])
                if c == 3:
                    S.add("pool", lambda e, ys=ys, h=h, b=b: e.dma_start(out=y_s[sl(8 + h, 128), sl(b, L)], in_=ystg[ys][:]), reads=[Bys[ys]], writes=[BY], dmakey="yst%d" % ys)

            stA(0)
            stA(1)
            stB(0)
            for i in range(16):
                if i + 1 < 16:
                    stB(i + 1)
                stC(i)
                if i + 2 < 16:
                    stA(i + 2)
            ws = wload(win_b[5], BW["in"])
            for j in range(16):
                bank = j % 2

                def mmV(e, ws=ws, j=j, bank=bank):
                    for k in range(8):
                        ins = e.matmul(pb[bank][:], xT[:, k, sl(j, 128)], wbf[ws][:, k, :], start=(k == 0), stop=(k == 7))
                    return ins
                S.add("pe", mmV, reads=[Bwbf[ws], BxT], writes=[PB[bank]])
                if j % 2 == 0:
                    S.add("act", lambda e, j=j, bank=bank: e.activation(Vt[:, j, :], pb[bank][:], AF.Copy), reads=[PB[bank]], writes=[BVt[j]])
                else:
                    S.add("dve", lambda e, j=j, bank=bank: e.tensor_copy(Vt[:, j, :], pb[bank][:]), reads=[PB[bank]], writes=[BVt[j]])
            if b == 0:
                S.add("sp", lambda e: e.dma_start(out=ropec[:], in_=ropec_d), writes=[Brope], dmakey="c1D")
                S.add("sp", lambda e: e.dma_start(out=ropes[:], in_=ropes_d), writes=[Brope], dmakey="c1D")
            wsg = [wload(win_b[3], BW["in"]), wload(win_b[4], BW["in"])]
            qitems = [(g, h, c) for g in range(2) for h in range(4) for c in range(4)]

            def qA(i):
                g, h, c = qitems[i]
                s = i % 2
                bank = 2 + s
                ws = wsg[g]

                def mm(e, ws=ws, h=h, c=c, bank=bank):
                    for k in range(8):
                        ins = e.matmul(pb[bank][:], wbf[ws][:, k, sl(h, 128)], xT[:, k, sl(c, 512)], start=(k == 0), stop=(k == 7))
                    return ins
                S.add("pe", mm, reads=[Bwbf[ws], BxT], writes=[PB[bank]])
                S.add("act", lambda e, s=s, bank=bank: e.activation(qf[s][:], pb[bank][:], AF.Copy), reads=[PB[bank]], writes=[Bqf[s]])

            def qB(i):
                g, h, c = qitems[i]
                s = i % 2
                bank2 = 4 + s
                S.add("pe", lambda e, s=s, bank2=bank2: e.matmul(pb[bank2][:], prot[:], qf[s][:], start=True, stop=True),
                      reads=[Bqf[s]] + CONST, writes=[PB[bank2]])
                S.add("pool", lambda e, s=s, c=c: e.tensor_tensor(r1_[s][:], qf[s][:], ropec[:, sl(c, 512)], ALU.mult),
                      reads=[Bqf[s], Brope], writes=[Br1[s]])
                S.add("dve", lambda e, s=s, c=c, bank2=bank2: e.tensor_tensor(r2_[s][:], pb[bank2][:], ropes[:, sl(c, 512)], ALU.mult),
                      reads=[PB[bank2], Brope], writes=[Br2[s]])
                if g == 0:
                    S.add("dve", lambda e, s=s, c=c, h=h: e.tensor_tensor(qk[0:64, h, sl(c, 512)], r1_[s][0:64, :], r2_[s][0:64, :], ALU.add),
                          reads=[Br1[s], Br2[s]], writes=[Bqk[h][c]])
                    S.add("dve", lambda e, s=s, c=c, h=h: e.tensor_tensor(qk[64:128, 4 + h, sl(c, 512)], r1_[s][64:128, :], r2_[s][64:128, :], ALU.add),
                          reads=[Br1[s], Br2[s]], writes=[Bqk[4 + h][c]])
                else:
                    S.add("dve", lambda e, s=s, c=c, h=h: e.tensor_tensor(qk[:, 8 + h, sl(c, 512)], r1_[s][:], r2_[s][:], ALU.add),
                          reads=[Br1[s], Br2[s]], writes=[Bqk[8 + h][c]])

            qA(0)
            for i in range(len(qitems)):
                if i + 1 < len(qitems):
                    qA(i + 1)
                qB(i)
            ie = 0
            pend = [None]
            for h in range(4):
                ys = iy[0] % 2
                iy[0] += 1
                for qc in range(4):
                    def add_S(kt, h=h, qc=qc):
                        p_ = kt % 2

                        def mms(e, kt=kt, h=h, qc=qc, p_=p_):
                            e.matmul(pb[4 + 2 * p_][:], qk[:, 8 + h, sl(kt, 128)], qk[:, h, sl(qc, 512)], start=True, stop=True)
                            return e.matmul(pb[5 + 2 * p_][:], qk[:, 8 + h, sl(kt, 128)], qk[:, 4 + h, sl(qc, 512)], start=True, stop=True)
                        S.add("pe", mms, reads=[Bqk[8 + h][kt // 4], Bqk[h][qc], Bqk[4 + h][qc]], writes=[PB[4 + 2 * p_], PB[5 + 2 * p_]])
                    add_S(0)
                    for kt in range(16):
                        p_ = kt % 2
                        es = ie % 4
                        ie += 1
                        S.add("act", lambda e, es=es, p_=p_: e.activation(E[es][:].rearrange("p (c q) -> p c q", c=2), pbig[:, 4 + 2 * p_:6 + 2 * p_, :], AF.Exp, scale=0.125),
                              reads=[PB[4 + 2 * p_], PB[5 + 2 * p_]], writes=[BE[es]])
                        if kt + 1 < 16:
                            add_S(kt + 1)

                        def mmo(e, h=h, kt=kt, es=es):
                            e.matmul(pb[0][:], Vt[:, kt, sl(h, 128)], E[es][:, 0:512], start=(kt == 0), stop=(kt == 15))
                            e.matmul(pb[2][:], onesb[:], E[es][:, 0:512], start=(kt == 0), stop=(kt == 15))
                            e.matmul(pb[1][:], Vt[:, kt, sl(h, 128)], E[es][:, 512:1024], start=(kt == 0), stop=(kt == 15))
                            return e.matmul(pb[3][:], onesb[:], E[es][:, 512:1024], start=(kt == 0), stop=(kt == 15))
                        S.add("pe", mmo, reads=[BVt[kt], BE[es]] + CONST, writes=[PB[0], PB[1], PB[2], PB[3]])
                    prev = pend[0]
                    if prev is not None:
                        S.add("pe", lambda e: e.matmul(pb[6][:], onesf[:], ep[5][:], start=True, stop=True), reads=[Bep[5]] + CONST, writes=[PB[6]])
                    S.add("act", lambda e: e.activation(epr[:].rearrange("p (c q) -> p c q", c=2), pbig[:, 2:4, :], AF.Ln), reads=[PB[2], PB[3]], writes=[Bep[0], Bep[1]])
                    S.add("act", lambda e: e.activation(epr[:], epr[:], AF.Exp, scale=-1.0), reads=[Bep[0], Bep[1]], writes=[Bep[0], Bep[1]])
                    S.add("dve", lambda e: e.tensor_tensor(ep[2][:], pb[0][:], ep[0][:], ALU.mult), reads=[PB[0], Bep[0]], writes=[Bep[2]])
                    S.add("dve", lambda e: e.tensor_tensor(ep[3][:], pb[1][:], ep[1][:], ALU.mult), reads=[PB[1], Bep[1]], writes=[Bep[3]])
                    if prev is not None:
                        prev()
                    S.add("dve", lambda e: e.scalar_tensor_tensor(ep[4][:], ep[3][:], neglam[:, 0:1], ep[2][:], ALU.mult, ALU.add),
                          reads=[Bep[2], Bep[3]] + CONST, writes=[Bep[4]])
                    S.add("pool", lambda e: e.tensor_tensor(ep[5][:], ep[4][:], ep[4][:], ALU.mult), reads=[Bep[4]], writes=[Bep[5]])

                    def tail(ys=ys, qc=qc, h=h, b=b):
                        S.add("act", lambda e: e.activation(ep[6][:], pb[6][:], AF.Ln, bias=epsr[:, 0:1], scale=1.0 / 128.0), reads=[PB[6]] + CONST, writes=[Bep[6]])
                        S.add("act", lambda e: e.activation(ep[6][:], ep[6][:], AF.Exp, scale=-0.5), reads=[Bep[6]], writes=[Bep[6]])
                        S.add("dve", lambda e, ys=ys, qc=qc: e.scalar_tensor_tensor(ystg[ys][:, sl(qc, 512)], ep[4][:], gsc[:, 0:1], ep[6][:], ALU.mult, ALU.mult),
                              reads=[Bep[4], Bep[6]] + CONST, writes=[Bys[ys]])
                        if qc == 3:
                            S.add("pool", lambda e, ys=ys, h=h, b=b: e.dma_start(out=y_s[sl(4 + h, 128), sl(b, L)], in_=ystg[ys][:]), reads=[Bys[ys]], writes=[BY], dmakey="yst%d" % ys)
                    pend[0] = tail
            S.add("pe", lambda e: e.matmul(pb[6][:], onesf[:], ep[5][:], start=True, stop=True), reads=[Bep[5]] + CONST, writes=[PB[6]])
            pend[0]()
            pend[0] = None

    BX1 = Buf("x1_s")
    BX1T = Buf("x1T_s")
    BwdnG = Buf("wdn")
    BlnG = Buf("ln")
    sharedC = {}

    def layer_norm_tail(rr, Brr, mv, Bmv, ntt, lnp, gi, Bln, sdt, pns=None, addb_eng="pool", after_tile=None, defer=False, mulg_eng="pool"):
        def stats():
            S.add("dve", lambda e: e.tensor_scalar_add(sdt[:, 0, 0:ntt], mv[:, 0:ntt, 1], 1e-5), reads=[Bmv], writes=[Bmv])
            S.add("act", lambda e: e.activation(sdt[:, 1, 0:ntt], sdt[:, 0, 0:ntt], AF.Sqrt), reads=[Bmv], writes=[Bmv])
            S.add("dve", lambda e: e.reciprocal(sdt[:, 2, 0:ntt], sdt[:, 1, 0:ntt]), reads=[Bmv], writes=[Bmv])
            S.add("dve", lambda e: e.scalar_tensor_tensor(sdt[:, 3, 0:ntt], mv[:, 0:ntt, 0], -1.0, sdt[:, 2, 0:ntt], ALU.mult, ALU.mult), reads=[Bmv], writes=[Bmv])

        def tile_fn(tt):
            def f():
                pn = 128 if pns is None else pns[tt]
                S.add("act", lambda e: e.activation(rr[0:pn, tt, :], rr[0:pn, tt, :], AF.Identity, bias=sdt[0:pn, 3, tt:tt + 1], scale=sdt[0:pn, 2, tt:tt + 1]),
                      reads=[Bmv, Brr[tt]], writes=[Brr[tt]])
                S.add(mulg_eng, lambda e: e.tensor_tensor(rr[0:pn, tt, :], rr[0:pn, tt, :], lnp[0:pn, gi, :], ALU.mult), reads=[Brr[tt], Bln], writes=[Brr[tt]])
                S.add(addb_eng, lambda e: e.tensor_tensor(rr[0:pn, tt, :], rr[0:pn, tt, :], lnp[0:pn, gi + 1, :], ALU.add), reads=[Brr[tt], Bln], writes=[Brr[tt]])
                if after_tile is not None:
                    after_tile(tt)
            return f
        fns = [stats] + [tile_fn(tt) for tt in range(ntt)]
        if defer:
            return fns
        for f in fns:
            f()

    def residual_stats(rr, Brr, tt, xres, Bxres, banks, mv, Bmv, bst, Bbst, pn=128):
        for nh in range(2):
            S.add("dve", lambda e, nh=nh: e.scalar_tensor_tensor(rr[0:pn, tt, sl(nh, 512)], xres[0:pn, sl(nh, 512)], ALPHA, pb[banks[nh]][0:pn, :], ALU.mult, ALU.add),
                  reads=[Bxres, PB[banks[nh]]], writes=[Brr[tt]])
        for nh in range(2):
            S.add("dve", lambda e, nh=nh: e.bn_stats(bst[0:pn, nh, :], rr[0:pn, tt, sl(nh, 512)]), reads=[Brr[tt], Bbst], writes=[Bbst])
        S.add("dve", lambda e: e.bn_aggr(mv[0:pn, tt, :], bst[0:pn]), reads=[Bbst, Bmv], writes=[Bmv])

    def phase_C1():
        S.fence()
        A.reset()
        wdn = A.alloc([128, 22, D], BF16)
        lnp = A.alloc([128, 4, D], F32)
        sharedC["wdn"], sharedC["lnp"], sharedC["off"] = wdn, lnp, A.off
        wo = A.alloc([128, 12, D], BF16)
        ych = [A.alloc([128, 12, 512], BF16) for _ in range(2)]
        xst = [A.alloc([128, D], F32) for _ in range(2)]
        rr = [A.alloc([128, 4, D], F32) for _ in range(2)]
        x1T = [A.alloc([128, 8, 512], BF16) for _ in range(2)]
        mv = [A.alloc([128, 4, 2], F32) for _ in range(2)]
        sdt = [A.alloc([128, 4, 4], F32) for _ in range(2)]
        bst = A.alloc([128, 2, 6], F32)
        Bwo = Buf("wo")
        Bln = BlnG
        S.add("sp", lambda e: e.dma_start(out=wo[:], in_=wout_b), reads=[BW["out"]], writes=[Bwo], dmakey="c1C1")
        S.add("sp", lambda e: e.dma_start(out=lnp[:], in_=ln_d), writes=[Bln], dmakey="c1C1")
        Bych = [Buf("ych0"), Buf("ych1")]
        Bxst = [Buf("xst0"), Buf("xst1")]
        Bx1T = [Buf("x1T0"), Buf("x1T1")]
        Bbst = Buf("bst")
        ixc = [0]
        Brr_all = [[Buf("rr") for _ in range(4)] for _ in range(2)]
        Bmv_all = [Buf("mv0"), Buf("mv1")]

        def stage1_begin(cg):
            cs = cg % 2
            S.add("sp", lambda e, cg=cg, cs=cs: e.dma_start(out=ych[cs][:], in_=y_s[:, sl(cg, 512)].rearrange("(k p) t -> p k t", p=128)),
                  reads=[BY], writes=[Bych[cs]], dmakey="yl%d" % cs)

        def stage1_tile(cg, tt):
            b, c = cg // 4, cg % 4
            cs = cg % 2
            Brr = Brr_all[cs]
            Bmv = Bmv_all[cs]
            xs_ = ixc[0] % 2
            ixc[0] += 1
            tok0 = c * 512 + tt * 128
            S.add("sp", lambda e, xs_=xs_, b=b, tok0=tok0: e.dma_start(out=xst[xs_][:], in_=x[b, tok0:tok0 + 128, :]), writes=[Bxst[xs_]], dmakey="xst%d" % xs_)
            banks = [(tt % 2) * 2, (tt % 2) * 2 + 1]
            for nh in range(2):
                def mm(e, cs=cs, tt=tt, nh=nh, bank=banks[nh]):
                    for k in range(12):
                        ins = e.matmul(pb[bank][:], ych[cs][:, k, sl(tt, 128)], wo[:, k, sl(nh, 512)], start=(k == 0), stop=(k == 11))
                    return ins
                S.add("pe", mm, reads=[Bych[cs], Bwo], writes=[PB[banks[nh]]])
            residual_stats(rr[cs], Brr, tt, xst[xs_], Bxst[xs_], banks, mv[cs], Bmv, bst, Bbst)

        def stage2_fns(cg):
            cs = cg % 2
            Brr = Brr_all[cs]

            def store(tt, cs=cs, cg=cg, Brr=Brr):
                g0 = cg * 512 + tt * 128
                S.add("pool", lambda e: e.dma_start(out=x1_s[g0:g0 + 128, :], in_=rr[cs][:, tt, :]), reads=[Brr[tt]], writes=[BX1], dmakey="x1st%d" % cs)
            return layer_norm_tail(rr[cs], Brr, mv[cs], Bmv_all[cs], 4, lnp, 0, Bln, sdt[cs], addb_eng="dve", mulg_eng="dve",
                                   after_tile=store, defer=True)

        def stage2_pe(cg, tt):
            cs = cg % 2
            Brr = Brr_all[cs]
            for h in range(2):
                bank = 4 + h + 2 * (tt % 2)

                def tr(e, cs=cs, tt=tt, h=h, bank=bank):
                    for q in range(4):
                        ins = e.transpose(pb[bank][:, sl(q, 128)], rr[cs][:, tt, sl(h * 4 + q, 128)], identf[:])
                    return ins
                S.add("pe", tr, reads=[Brr[tt]] + CONST, writes=[PB[bank]])
                dst = x1T[cs][:, h * 4:(h + 1) * 4, sl(tt, 128)]
                src_ = pb[bank][:].rearrange("p (q t) -> p q t", q=4)
                if h == 0:
                    S.add("act", lambda e, dst=dst, src_=src_: e.activation(dst, src_, AF.Copy), reads=[PB[bank]], writes=[Bx1T[cs]])
                else:
                    S.add("dve", lambda e, dst=dst, src_=src_: e.tensor_copy(dst, src_), reads=[PB[bank]], writes=[Bx1T[cs]])
            if tt == 3:
                S.add("pool", lambda e, cs=cs, cg=cg: e.dma_start(out=x1T_s[:, :, sl(cg, 512)], in_=x1T[cs][:]), reads=[Bx1T[cs]], writes=[BX1T], dmakey="x1Tst%d" % cs)

        stage1_begin(0)
        for tt in range(4):
            stage1_tile(0, tt)
        S.add("sp", lambda e: e.dma_start(out=wdn[:], in_=wdn_b), reads=[BW["down"]], writes=[BwdnG], dmakey="c1C1w")
        for cg in range(1, 9):
            fns = stage2_fns(cg - 1)
            if cg < 8:
                stage1_begin(cg)
            fns[0]()
            for tt in range(4):
                fns[1 + tt]()
                if cg < 8:
                    stage1_tile(cg, tt)
                stage2_pe(cg - 1, tt)


    def phase_C2():
        S.fence()
        A.reset(sharedC["off"])
        wdn, lnp = sharedC["wdn"], sharedC["lnp"]
        xw = [A.alloc([128, 8, 512], BF16) for _ in range(2)]
        wup = [A.alloc([128, 8, 2, 128], BF16) for _ in range(3)]
        t1 = [A.alloc([128, 512], F32) for _ in range(4)]
        hh = [A.alloc([128, 512], F32) for _ in range(4)]
        sg = [A.alloc([128, 512], F32) for _ in range(2)]
        act = A.alloc([128, 22, 512], BF16)
        x1t = [A.alloc([128, D], F32) for _ in range(2)]
        rr = [A.alloc([128, 4, D], F32) for _ in range(2)]
        mv = [A.alloc([128, 4, 2], F32) for _ in range(2)]
        sdt = [A.alloc([128, 4, 4], F32) for _ in range(2)]
        bst = A.alloc([128, 2, 6], F32)
        Bwdn = BwdnG
        Bln = BlnG
        Bxw = [Buf("xw0"), Buf("xw1")]
        Bwup = [Buf("wup%d" % i) for i in range(3)]
        Bt1 = [Buf("t1%d" % i) for i in range(4)]
        Bhh = [Buf("hh%d" % i) for i in range(4)]
        Bsg = [Buf("sg0"), Buf("sg1")]
        Bx1t = [Buf("x1t0"), Buf("x1t1")]
        Bbst = Buf("bst")
        Bmv_all = [Buf("mv0"), Buf("mv1")]
        for i_ in range(2):
            S.add("pool", lambda e, i_=i_: e.memset(mv[i_][:], 1.0), writes=[Bmv_all[i_]])
            S.add("pool", lambda e, i_=i_: e.memset(sdt[i_][:], 1.0), writes=[Bmv_all[i_]])
        iw = [0]

        def wuload(m):
            s = iw[0] % 3
            iw[0] += 1
            S.add("sp", lambda e: e.dma_start(out=wup[s][:], in_=wup_b[m]), reads=[BW["up"]], writes=[Bwup[s]], dmakey="wu%d" % s)
            return s
        ig = 0
        ix = 0
        Bact = [Buf("act") for _ in range(22)]
        Brr_all = [[Buf("rr") for _ in range(4)] for _ in range(2)]
        chunks = []
        for b in range(NB):
            for c0, n_ in ((0, 510), (510, 510), (1020, 510), (1530, 262), (1792, 256)):
                chunks.append((b, c0, n_))
        pendC = []
        for cgi, (b, c0, n) in enumerate(chunks):
            cs = cgi % 2
            g0 = b * L + c0
            N = n + 2
            lo = 1 if c0 == 0 else 0
            hi = n + 1 if c0 + n == L else n + 2
            S.add("sp", lambda e, cs=cs, g0=g0, lo=lo, hi=hi: e.dma_start(out=xw[cs][:, :, lo:hi], in_=x1T_s[:, :, g0 - 1 + lo:g0 - 1 + hi]),
                  reads=[BX1T], writes=[Bxw[cs]], dmakey="xw%d" % cs)
            if c0 == 0:
                S.add("pool", lambda e, cs=cs: e.memset(xw[cs][:, :, 0:1], 0.0), writes=[Bxw[cs]])
            if c0 + n == L:
                S.add("pool", lambda e, cs=cs, n=n: e.memset(xw[cs][:, :, n + 1:n + 2], 0.0), writes=[Bxw[cs]])
            nxt = wuload(0)
            for m in range(22):
                ws = nxt
                if m + 1 < 22:
                    nxt = wuload(m + 1)
                hsl = []
                for gu in range(2):
                    s = ig % 4
                    ig += 1
                    bank = s

                    def mm(e, ws=ws, gu=gu, cs=cs, bank=bank, N=N):
                        for k in range(8):
                            ins = e.matmul(pb[bank][:, 0:N], wup[ws][:, k, gu, :], xw[cs][:, k, 0:N], start=(k == 0), stop=(k == 7))
                        return ins
                    S.add("pe", mm, reads=[Bwup[ws], Bxw[cs]], writes=[PB[bank]])
                    fi = gu * 22 + m
                    S.add("act", lambda e, s=s, fi=fi, bank=bank, n=n: e.activation(t1[s][:, 0:n], pb[bank][:, 1:n + 1], AF.Identity, bias=fcb[:, fi:fi + 1], scale=fcw[:, fi, 1:2]),
                          reads=[PB[bank]] + CONST, writes=[Bt1[s]])
                    S.add("dve", lambda e, s=s, fi=fi, bank=bank, n=n: e.scalar_tensor_tensor(t1[s][:, 0:n], pb[bank][:, 0:n], fcw[:, fi, 0:1], t1[s][:, 0:n], ALU.mult, ALU.add),
                          reads=[PB[bank], Bt1[s]] + CONST, writes=[Bt1[s]])
                    S.add("dve", lambda e, s=s, fi=fi, bank=bank, n=n: e.scalar_tensor_tensor(hh[s][:, 0:n], pb[bank][:, 2:n + 2], fcw[:, fi, 2:3], t1[s][:, 0:n], ALU.mult, ALU.add),
                          reads=[PB[bank], Bt1[s]] + CONST, writes=[Bhh[s]])
                    hsl.append(s)
                if pendC and m in (2, 6, 10, 14, 18):
                    pendC.pop(0)()
                ss = m % 2
                S.add("act", lambda e, ss=ss, s0=hsl[0], n=n: e.activation(sg[ss][:, 0:n], hh[s0][:, 0:n], AF.Silu), reads=[Bhh[hsl[0]]], writes=[Bsg[ss]])
                S.add("pool", lambda e, ss=ss, s1=hsl[1], m=m, n=n: e.tensor_tensor(act[:, m, 0:n], sg[ss][:, 0:n], hh[s1][:, 0:n], ALU.mult),
                      reads=[Bsg[ss], Bhh[hsl[1]]], writes=[Bact[m]])
            Brr = Brr_all[cs]
            Bmv = Bmv_all[cs]
            tiles = [(ts, min(128, n - ts)) for ts in range(0, n, 128)]
            for tt, (ts, tn) in enumerate(tiles):
                xs_ = ix % 2
                ix += 1
                S.add("sp", lambda e, xs_=xs_, g0=g0, ts=ts, tn=tn: e.dma_start(out=x1t[xs_][0:tn, :], in_=x1_s[g0 + ts:g0 + ts + tn, :]),
                      reads=[BX1], writes=[Bx1t[xs_]], dmakey="x1l%d" % xs_)
                banks = [6, 7] if tt % 2 == 0 else [4, 5]
                for nh in range(2):
                    def mm(e, ts=ts, tn=tn, nh=nh, bank=banks[nh]):
                        for k in range(22):
                            ins = e.matmul(pb[bank][0:tn, :], act[:, k, ts:ts + tn], wdn[:, k, sl(nh, 512)], start=(k == 0), stop=(k == 21))
                        return ins
                    S.add("pe", mm, reads=Bact + [Bwdn], writes=[PB[banks[nh]]])
                residual_stats(rr[cs], Brr, tt, x1t[xs_], Bx1t[xs_], banks, mv[cs], Bmv, bst, Bbst, pn=tn)
            while pendC:
                pendC.pop(0)()

            def store_tile(tt, cs=cs, b=b, c0=c0, tiles=tiles, Brr=Brr):
                ts, tn = tiles[tt]
                finals.append(S.add("pool", lambda e: e.dma_start(out=out[b, c0 + ts:c0 + ts + tn, :], in_=rr[cs][0:tn, tt, :]),
                                    reads=[Brr[tt]], dmakey="ost%d" % cs))
            pendC.extend(layer_norm_tail(rr[cs], Brr, mv[cs], Bmv, len(tiles), lnp, 2, Bln, sdt[cs], pns=[tn for _, tn in tiles],
                                         after_tile=store_tile, defer=True))
        while pendC:
            pendC.pop(0)()

    phase_A()
    phase_W()
    if stop_after != "A":
        phase_H()
        if stop_after not in ("H1", "H"):
            phase_D()
            if stop_after != "D":
                phase_C1()
                if stop_after != "C1":
                    phase_C2()
    if not finals:
        finals.extend(S.last.values())
    S.emit(final_waits=finals)
    return nc


_CONST = None


def _constants():
    global _CONST
    if _CONST is not None:
        return _CONST
    bf = ml_dtypes.bfloat16
    c = {}
    c["identf"] = np.eye(128, dtype=np.float32)
    c["identb"] = np.eye(128).astype(bf)
    P = np.zeros((128, 128), np.float32)
    for m in range(128):
        i = m % 64
        if i < 32:
            P[m + 32, m] = -1.0
        else:
            P[m - 32, m] = 1.0
    c["prot"] = P
    inv_freq = (10000.0 ** (-np.arange(0, 64, 2, dtype=np.float32) / 64)).astype(np.float32)
    ang = np.arange(L, dtype=np.float32)[:, None] * inv_freq[None, :]
    ang = np.concatenate([ang, ang], -1)
    cosT = np.cos(ang).astype(np.float32).T
    sinT = np.sin(ang).astype(np.float32).T
    c["ropec"] = np.ascontiguousarray(np.concatenate([cosT, cosT], 0))
    c["ropes"] = np.ascontiguousarray(np.concatenate([sinT, sinT], 0))
    t = np.linspace(0.0, 1.0, L, dtype=np.float32)[:, None]
    fr = np.linspace(1e-4, 15, 16, dtype=np.float32)[None, :]
    w = (2.0 * math.pi * np.arange(L, dtype=np.float32)[:, None] / L).astype(np.float32)
    z = np.concatenate([t, np.cos(fr * w), -np.sin(fr * w)], -1).astype(np.float32)
    c["zT"] = np.ascontiguousarray(z.T)
    dmin = math.log(1e-2) / 0.3
    dmax = math.log(1e-2) / 1.5
    deltas = np.abs(np.linspace(dmin, dmax, HW, dtype=np.float32))
    decay = np.exp(-t * deltas[None, :]).astype(np.float32)
    c["decay"] = np.ascontiguousarray(decay.reshape(16, 128, HW).transpose(1, 0, 2))
    n = np.arange(2048, dtype=np.int64)
    prod = (n[:, None] * n[None, :]) % NFFT
    angm = 2.0 * np.pi * prod.astype(np.float64) / NFFT
    Gc = np.cos(angm)
    Gs = -np.sin(angm)
    GsF = Gs.copy()
    GsF[:, 0] = (-1.0) ** n
    GsI = Gs.copy()
    GsI[0, :] = (-1.0) ** n

    def tile4(G):
        return np.ascontiguousarray(G.reshape(16, 128, 16, 128).transpose(2, 1, 0, 3)).astype(bf)
    c["gc"] = tile4(Gc)
    c["gsf"] = tile4(GsF)
    c["gsi"] = tile4(GsI)
    _CONST = c
    return c


def _prep_shared(inp):
    f = np.float32
    d = {}
    d["w_in"] = np.ascontiguousarray(inp["w_in"][0], f)
    d["w_kv"] = np.ascontiguousarray(inp["mem_w_kv"][0], f)
    d["w_out"] = np.ascontiguousarray(inp["w_out"][0], f)
    d["w_up"] = np.ascontiguousarray(inp["ffn_w_up"][0], f)
    d["w_down"] = np.ascontiguousarray(inp["ffn_w_down"][0], f)
    d["hcw"] = np.ascontiguousarray(np.asarray(inp["hy_conv_w"][0], f).reshape(3, 12, 128).transpose(2, 1, 0))
    d["hcb"] = np.ascontiguousarray(np.asarray(inp["hy_conv_b"][0], f).reshape(12, 128).T)
    d["fcw"] = np.ascontiguousarray(np.asarray(inp["ffn_conv_w"][0], f).reshape(3, 44, 128).transpose(2, 1, 0))
    d["fcb"] = np.ascontiguousarray(np.asarray(inp["ffn_conv_b"][0], f).reshape(44, 128).T)
    d["w1"] = np.ascontiguousarray(inp["hy_w1"][0], f)
    d["w2"] = np.ascontiguousarray(inp["hy_w2"][0], f)
    d["w3"] = np.ascontiguousarray(inp["hy_w3"][0], f)
    d["fv"] = np.ascontiguousarray(np.stack([inp["hy_b1"][0], inp["hy_b2"][0], inp["hy_freq"][0]], -1), f)
    d["hbias"] = np.ascontiguousarray(np.broadcast_to(np.asarray(inp["hy_bias"][0], f)[None], (128, 2, 512)))
    d["lam"] = np.ascontiguousarray(np.broadcast_to(np.asarray(inp["diff_lambda"][0], f).reshape(1, 256), (128, 256)))
    d["subg"] = np.ascontiguousarray(np.asarray(inp["diff_subln_g"][0], f).reshape(128, 1))
    lnp = np.stack([inp["ln1_g"][0], inp["ln1_b"][0], inp["ln2_g"][0], inp["ln2_b"][0]], 0).astype(f)
    d["lnp"] = np.ascontiguousarray(np.broadcast_to(lnp[None], (128, 4, 1024)))
    d.update(_constants())
    return d


_NC_CACHE = {}


def kernel(**inputs):
    inp = {k: np.asarray(v) for k, v in inputs.items()}
    shared = _prep_shared(inp)
    x = np.ascontiguousarray(inp["x"], np.float32)
    mem = np.ascontiguousarray(inp["mem"], np.float32)
    if "nc" not in _NC_CACHE:
        _NC_CACHE["nc"] = build()
    nc = _NC_CACHE["nc"]
    in_maps = []
    for i in range(NCORES):
        m = dict(shared)
        m["x"] = x[i * NB:(i + 1) * NB]
        m["mem"] = mem[i * NB:(i + 1) * NB]
        in_maps.append(m)
    res = run_bass_kernel_spmd(nc, in_maps, core_ids=list(range(NCORES)))
    outs = [np.asarray(r["out"], np.float32) for r in res.results]
    return np.concatenate(outs, axis=0)
```

```python
import math
import contextlib
import numpy as np
import ml_dtypes
import concourse.bass as bass
import concourse.mybir as mybir
from concourse.bass_utils import run_bass_kernel_spmd

F32 = mybir.dt.float32
BF16 = mybir.dt.bfloat16
ALU = mybir.AluOpType
AF = mybir.ActivationFunctionType

NCORES = 8
NB = 2
L = 2048
D = 1024
T = NB * L
HW = 512
DFF = 2816
ALPHA = 2.0 ** 0.25
LAMBDA_INIT = 0.8 - 0.6 * math.exp(0.0)
NFFT = 4096
ENGS = ("pe", "act", "dve", "pool", "sp")
SB_BASE = 16512
SB_END = 229376


class Buf:
    __slots__ = ("name", "w", "r")

    def __init__(self, name=""):
        self.name = name
        self.w = None
        self.r = []


class Op:
    __slots__ = ("eng", "fn", "deps", "signal", "key", "count", "ord")

    def __init__(self, eng, fn, key):
        self.eng = eng
        self.fn = fn
        self.deps = {}
        self.signal = False
        self.key = key
        self.count = None
        self.ord = None


class Sched:
    def __init__(self, nc):
        self.nc = nc
        self.ops = {e: [] for e in ENGS}
        self.nord = {}
        self.allops = []
        self.last = {}
        self.fence_ops = {}

    def _dep(self, op, d, fence=False):
        if d is None or d is op:
            return
        if d.key == "pe" and op.eng == "pe":
            return
        if d.key not in ENGS and not fence:
            d = self.last[d.key]
        cur = op.deps.get(d.key)
        if cur is None or cur.ord < d.ord:
            op.deps[d.key] = d

    def fence(self):
        self.fence_ops = {k: v for k, v in self.last.items() if not str(k).startswith("cv_")}

    def add(self, eng, fn, reads=(), writes=(), dmakey=None):
        key = dmakey if dmakey is not None else eng
        op = Op(eng, fn, key)
        op.ord = self.nord.get(key, 0)
        self.nord[key] = op.ord + 1
        for d in self.fence_ops.values():
            self._dep(op, d, fence=True)
        for b in reads:
            self._dep(op, b.w)
        for b in writes:
            self._dep(op, b.w)
            for r in b.r:
                self._dep(op, r)
        for b in reads:
            b.r.append(op)
        for b in writes:
            b.w = op
            b.r = []
        self.ops[eng].append(op)
        self.allops.append(op)
        self.last[key] = op
        return op

    def emit(self, final_waits=()):
        nc = self.nc
        for op in self.allops:
            for d in op.deps.values():
                d.signal = True
        for op in final_waits:
            op.signal = True
        for op in self.allops:
            if op.key not in ENGS:
                op.signal = True
        cnt = {}
        for op in self.allops:
            if op.signal:
                inc = 1 if op.key in ENGS else 16
                cnt[op.key] = cnt.get(op.key, 0) + inc
                op.count = cnt[op.key]
        with contextlib.ExitStack() as es:
            sems = {}
            for k in cnt:
                sems[k] = es.enter_context(nc.semaphore("s_" + str(k)))
            block = es.enter_context(nc.Block())

            def run(engname):
                def body(eng):
                    waited = {}
                    for op in self.ops[engname]:
                        for k, d in op.deps.items():
                            if waited.get(k, 0) < d.count:
                                eng.wait_ge(sems[k], d.count)
                                waited[k] = d.count
                        ins = op.fn(eng)
                        if op.signal:
                            ins.then_inc(sems[op.key], 1 if op.key in ENGS else 16)
                    if engname == "sp":
                        for op in final_waits:
                            if waited.get(op.key, 0) < op.count:
                                eng.wait_ge(sems[op.key], op.count)
                                waited[op.key] = op.count
                return body

            block.tensor(run("pe"))
            block.scalar(run("act"))
            block.vector(run("dve"))
            block.gpsimd(run("pool"))
            block.sync(run("sp"))


class Arena:
    def __init__(self, nc, base, limit):
        self.nc = nc
        self.base = base
        self.off = base
        self.limit = limit
        self.n = 0

    def reset(self, to=None):
        self.off = self.base if to is None else to

    def alloc(self, shape, dtype):
        n = 1
        for s in shape[1:]:
            n *= s
        nbytes = n * (4 if dtype == F32 else 2)
        nbytes = (nbytes + 63) // 64 * 64
        off = self.off
        self.off += nbytes
        assert self.off <= self.limit, ("SBUF overflow", self.off, self.limit)
        self.n += 1
        return self.nc.alloc_sbuf_tensor_at("a%d" % self.n, list(shape), dtype, offset=off)


def sl(i, n):
    return slice(i * n, (i + 1) * n)


def build(dbg=False, stop_after=None):
    nc = bass.Bass("TRN2", target_bir_lowering=False)
    S = Sched(nc)
    skind = "ExternalOutput" if dbg else "Internal"

    def din(name, shape, dt=F32):
        return nc.dram_tensor(name, list(shape), dt, kind="ExternalInput").ap()

    def dscr(name, shape, dt):
        return nc.dram_tensor(name, list(shape), dt, kind=skind).ap()

    x = din("x", [NB, L, D])
    mem = din("mem", [NB, 256, D])
    w_in = din("w_in", [D, 3584])
    w_kv = din("w_kv", [D, 1024])
    w_out = din("w_out", [1536, D])
    w_up = din("w_up", [D, 2 * DFF])
    w_down = din("w_down", [DFF, D])
    hcw_d = din("hcw", [128, 12, 3])
    hcb_d = din("hcb", [128, 12])
    fcw_d = din("fcw", [128, 44, 3])
    fcb_d = din("fcb", [128, 44])
    w1_d = din("w1", [33, 64])
    w2_d = din("w2", [64, 64])
    w3_d = din("w3", [64, 2048])
    fv_d = din("fv", [64, 3])
    hb_d = din("hbias", [128, 2, 512])
    lam_d = din("lam", [128, 256])
    subg_d = din("subg", [128, 1])
    ln_d = din("lnp", [128, 4, 1024])
    identf_d = din("identf", [128, 128])
    identb_d = din("identb", [128, 128], BF16)
    prot_d = din("prot", [128, 128])
    ropec_d = din("ropec", [128, L])
    ropes_d = din("ropes", [128, L])
    zT_d = din("zT", [33, L])
    dec_d = din("decay", [128, 16, 512])
    gc_d = din("gc", [16, 128, 16, 128], BF16)
    gsf_d = din("gsf", [16, 128, 16, 128], BF16)
    gsi_d = din("gsi", [16, 128, 16, 128], BF16)
    out = nc.dram_tensor("out", [NB, L, D], F32, kind="ExternalOutput").ap()

    win_b = dscr("win_b", [7, 128, 8, 512], BF16)
    wkv_b = dscr("wkv_b", [128, 8, 1024], BF16)
    wout_b = dscr("wout_b", [128, 12, 1024], BF16)
    wup_b = dscr("wup_b", [22, 128, 8, 2, 128], BF16)
    wdn_b = dscr("wdn_b", [128, 22, 1024], BF16)
    kf_s = dscr("kf_s", [16, 128, 2, 2, 512], BF16)
    xT_s = dscr("xT_s", [NB, 128, 8, L], BF16)
    y_s = dscr("y_s", [1536, T], BF16)
    x1_s = dscr("x1_s", [T, D], F32)
    x1T_s = dscr("x1T_s", [128, 8, T], BF16)

    pbig = nc.alloc_psum_tensor("pbig", [128, 8, 512], F32)
    pb = [pbig[:, i, :] for i in range(8)]
    PB = [Buf("pb%d" % i) for i in range(8)]

    per = Arena(nc, SB_BASE, SB_BASE + 6144)
    identf = per.alloc([128, 128], F32)
    identb = per.alloc([128, 128], BF16)
    prot = per.alloc([128, 128], F32)
    onesb = per.alloc([128, 128], BF16)
    onesf = per.alloc([128, 128], F32)
    hcw = per.alloc([128, 12, 3], F32)
    hcb = per.alloc([128, 12], F32)
    fcw = per.alloc([128, 44, 3], F32)
    fcb = per.alloc([128, 44], F32)
    subg = per.alloc([128, 1], F32)
    gsc = per.alloc([128, 1], F32)
    neglam = per.alloc([128, 1], F32)
    epsr = per.alloc([128, 1], F32)
    lamt = per.alloc([128, 8], F32)
    lamp = per.alloc([128, 256], F32)
    A = Arena(nc, per.off, SB_END)
    BC = Buf("consts")

    def cload(dst, src):
        S.add("sp", lambda e: e.dma_start(out=dst, in_=src), writes=[BC], dmakey="c0")

    cload(identf[:], identf_d)
    cload(identb[:], identb_d)
    cload(prot[:], prot_d)
    cload(hcw[:], hcw_d)
    cload(hcb[:], hcb_d)
    cload(fcw[:], fcw_d)
    cload(fcb[:], fcb_d)
    cload(subg[:], subg_d)
    cload(lamp[:], lam_d)
    BC2 = Buf("consts2")
    S.add("pool", lambda e: e.memset(onesb[:], 1.0), writes=[BC2])
    S.add("pool", lambda e: e.memset(onesf[:], 1.0), writes=[BC2])
    S.add("pool", lambda e: e.memset(epsr[:], 1e-5), writes=[BC2])
    S.add("dve", lambda e: e.tensor_tensor(lamp[:, 0:64], lamp[:, 0:64], lamp[:, 64:128], ALU.mult), reads=[BC], writes=[BC2])
    S.add("dve", lambda e: e.tensor_tensor(lamp[:, 128:192], lamp[:, 128:192], lamp[:, 192:256], ALU.mult), reads=[BC2], writes=[BC2])
    S.add("dve", lambda e: e.reduce_sum(lamt[:, 0:1], lamp[:, 0:64], mybir.AxisListType.X), reads=[BC2], writes=[BC2])
    S.add("dve", lambda e: e.reduce_sum(lamt[:, 1:2], lamp[:, 128:192], mybir.AxisListType.X), reads=[BC2], writes=[BC2])
    S.add("act", lambda e: e.activation(lamt[:, 2:4], lamt[:, 0:2], AF.Exp), reads=[BC2], writes=[BC2])
    S.add("dve", lambda e: e.scalar_tensor_tensor(neglam[:], lamt[:, 3:4], -LAMBDA_INIT, lamt[:, 2:3], ALU.add, ALU.subtract), reads=[BC2], writes=[BC2])
    S.add("dve", lambda e: e.tensor_scalar_mul(gsc[:], subg[:], 1.0 - LAMBDA_INIT), reads=[BC, BC2], writes=[BC2])
    CONST = [BC, BC2]

    BW = {k: Buf(k) for k in ("in", "kv", "out", "up", "down")}

    def phase_W():
      for k in range(8):
        S.add("pool", lambda e, k=k: e.dma_start(out=win_b.rearrange("g p k c -> p g k c")[:, :, k, :],
                                                  in_=w_in[sl(k, 128), :].rearrange("p (g c) -> p g c", g=7)),
              writes=[BW["in"]], dmakey="cv_in")
      for k in range(8):
        S.add("pool", lambda e, k=k: e.dma_start(out=wkv_b[:, k, :], in_=w_kv[sl(k, 128), :]), writes=[BW["kv"]], dmakey="cv_kv")
      for k in range(12):
        S.add("pool", lambda e, k=k: e.dma_start(out=wout_b[:, k, :], in_=w_out[sl(k, 128), :]), writes=[BW["out"]], dmakey="cv_out")
      for k in range(8):
        for g in range(2):
            S.add("pool", lambda e, k=k, g=g: e.dma_start(
                out=wup_b.rearrange("m p k g c -> p m k g c")[:, :, k, g, :],
                in_=w_up[sl(k, 128), g * DFF:(g + 1) * DFF].rearrange("p (m c) -> p m c", m=22)),
                writes=[BW["up"]], dmakey="cv_up")
      for k in range(22):
        S.add("pool", lambda e, k=k: e.dma_start(out=wdn_b[:, k, :], in_=w_down[sl(k, 128), :]), writes=[BW["down"]], dmakey="cv_down")

    finals = []
    BKF = Buf("kf_s")

    def phase_A():
        A.reset()
        zT = A.alloc([33, L], F32)
        w1 = A.alloc([33, 64], F32)
        w2 = A.alloc([64, 64], F32)
        w3 = A.alloc([64, 2048], F32)
        w3n = A.alloc([64, 1024], BF16)
        w3h = A.alloc([64, 2048], BF16)
        h2h = A.alloc([64, L], BF16)
        fv = A.alloc([64, 3], F32)
        fb = A.alloc([64, 2], F32)
        hT = [A.alloc([64, L], F32) for _ in range(2)]
        h2b = A.alloc([64, L], BF16)
        arg = [A.alloc([64, 512], F32) for _ in range(2)]
        kk = [A.alloc([64, 512], F32) for _ in range(2)]
        dec = A.alloc([128, 16, 512], F32)
        hsd = A.alloc([128, 16, 2, 2, 512], BF16)
        gcb = [A.alloc([128, 16, 128], BF16) for _ in range(2)]
        gsb = [A.alloc([128, 16, 128], BF16) for _ in range(2)]
        kst = [A.alloc([128, 2, 2, 512], BF16) for _ in range(2)]
        BA = Buf("Aconst")
        for dst, src in ((zT[:], zT_d), (w1[:], w1_d), (w2[:], w2_d), (w3[:], w3_d), (fv[:], fv_d), (dec[:], dec_d)):
            S.add("sp", lambda e, dst=dst, src=src: e.dma_start(out=dst, in_=src), writes=[BA], dmakey="c1A")
        Bfb = Buf("fb")
        S.add("dve", lambda e: e.tensor_tensor(fb[:, 0:1], fv[:, 0:1], fv[:, 2:3], ALU.mult), reads=[BA], writes=[Bfb])
        S.add("dve", lambda e: e.tensor_tensor(fb[:, 1:2], fv[:, 1:2], fv[:, 2:3], ALU.mult), reads=[BA, Bfb], writes=[Bfb])
        Bw3n = Buf("w3n")
        for o in range(2):
            S.add("pool", lambda e, o=o: e.tensor_scalar_mul(w3n[:, sl(o, 512)], w3[:, o * 1024 + 512:(o + 1) * 1024], -1.0),
                  reads=[BA, Bw3n], writes=[Bw3n])
        S.add("pool", lambda e: e.tensor_copy(w3h[:], w3[:]), reads=[BA, Bw3n], writes=[Bw3n])
        BhT = [[Buf("hT%d_%d" % (l, c)) for c in range(4)] for l in range(2)]
        Barg = [Buf("arg0"), Buf("arg1")]
        Bkk = [Buf("kk0"), Buf("kk1")]
        MAG = 12582912.0
        for layer in range(2):
            lw = w1 if layer == 0 else w2
            for c in range(4):
                rhs = zT[:, sl(c, 512)] if layer == 0 else hT[0][:, sl(c, 512)]
                rb = [BA] if layer == 0 else [BhT[0][c]]
                S.add("pe", lambda e, lw=lw, rhs=rhs, c=c: e.matmul(pb[c][0:64, :], lw[:], rhs, start=True, stop=True),
                      reads=[BA] + rb, writes=[PB[c]])
                s = c % 2
                S.add("dve", lambda e, c=c, s=s, layer=layer: e.tensor_scalar(arg[s][:], pb[c][0:64, :], fv[:, 2:3], fb[:, layer:layer + 1], ALU.mult, ALU.add),
                      reads=[PB[c], BA, Bfb], writes=[Barg[s]])
                S.add("dve", lambda e, s=s: e.tensor_scalar(kk[s][:], arg[s][:], 1.0 / (2 * math.pi), MAG, ALU.mult, ALU.add),
                      reads=[Barg[s]], writes=[Bkk[s]])
                S.add("dve", lambda e, s=s: e.tensor_scalar(kk[s][:], kk[s][:], -MAG, -2 * math.pi, ALU.add, ALU.mult),
                      reads=[Bkk[s]], writes=[Bkk[s]])
                S.add("dve", lambda e, s=s: e.tensor_tensor(arg[s][:], arg[s][:], kk[s][:], ALU.add),
                      reads=[Barg[s], Bkk[s]], writes=[Barg[s]])
                S.add("act", lambda e, s=s, c=c, layer=layer: e.activation(hT[layer][:, sl(c, 512)], arg[s][:], AF.Sin, scale=0.999999),
                      reads=[Barg[s]], writes=[BhT[layer][c]])
        Bh2b = Buf("h2b")
        S.add("pool", lambda e: e.tensor_copy(h2b[:], hT[1][:]), reads=BhT[1], writes=[Bh2b])
        S.add("pool", lambda e: e.memset(h2b[:, 0:1], 0.0), reads=[Bh2b], writes=[Bh2b])
        S.add("dve", lambda e: e.tensor_copy(h2h[:], hT[1][:]), reads=BhT[1], writes=[Bh2b])
        Bhsd = [Buf("hsd%d" % j) for j in range(16)]
        ib = 0
        for j in range(16):
            for o in range(2):
                for part in range(2):
                    bank = 4 + (ib % 4)
                    ib += 1
                    wb_ = w3h[:, o * 1024 + 512:(o + 1) * 1024] if part == 0 else w3n[:, sl(o, 512)]

                    def mmf(e, bank=bank, j=j, o=o, wb_=wb_):
                        e.matmul(pb[bank][:], h2h[:, sl(j, 128)], w3h[:, o * 1024:o * 1024 + 512], start=True, stop=False)
                        return e.matmul(pb[bank][:], h2b[:, sl(j, 128)], wb_, start=False, stop=True)
                    S.add("pe", mmf, reads=[BA, Bw3n, Bh2b] + BhT[1], writes=[PB[bank]])
                    S.add("dve", lambda e, bank=bank, j=j, o=o, part=part: e.tensor_tensor(hsd[:, j, part, o, :], pb[bank][:], dec[:, j, :], ALU.mult),
                          reads=[PB[bank], BA], writes=[Bhsd[j]])
        Bg = [Buf("gA0"), Buf("gA1")]
        Bg2 = [Buf("gB0"), Buf("gB1")]
        Bkst = [Buf("kst0"), Buf("kst1")]

        def gload(m):
            s = m % 2
            S.add("sp", lambda e: e.dma_start(out=gcb[s][:], in_=gc_d[m]), writes=[Bg[s]], dmakey="g%d" % s)
            S.add("sp", lambda e: e.dma_start(out=gsb[s][:], in_=gsf_d[m]), writes=[Bg2[s]], dmakey="gs%d" % s)
        gload(0)
        ib = 0
        for m in range(16):
            if m + 1 < 16:
                gload(m + 1)
            s = m % 2
            for part in range(2):
                g = gcb[s] if part == 0 else gsb[s]
                for o in range(2):
                    bank = ib % 4
                    ib += 1

                    def mms(e, bank=bank, g=g, part=part, o=o):
                        for k in range(16):
                            ins = e.matmul(pb[bank][:], g[:, k, :], hsd[:, k, part, o, :], start=(k == 0), stop=(k == 15))
                        return ins
                    S.add("pe", mms, reads=[Bg[s], Bg2[s]] + Bhsd, writes=[PB[bank]])
                    S.add("act", lambda e, bank=bank, s=s, o=o, part=part: e.activation(kst[s][:, o, part, :], pb[bank][:], AF.Copy),
                          reads=[PB[bank]], writes=[Bkst[s]])
            if m == 0:
                for o in range(2):
                    bank = ib % 4
                    ib += 1

                    def mmn(e, bank=bank, o=o, s=s):
                        for k in range(16):
                            ins = e.matmul(pb[bank][0:1, :], gsb[s][:, k, 0:1], hsd[:, k, 0, o, :], start=(k == 0), stop=(k == 15))
                        return ins
                    S.add("pe", mmn, reads=[Bg[s], Bg2[s]] + Bhsd, writes=[PB[bank]])
                    S.add("act", lambda e, bank=bank, s=s, o=o: e.activation(kst[s][0:1, o, 1, :], pb[bank][0:1, :], AF.Copy),
                          reads=[PB[bank]], writes=[Bkst[s]])
            S.add("sp", lambda e, m=m, s=s: e.dma_start(out=kf_s[m], in_=kst[s][:]), reads=[Bkst[s]], writes=[BKF], dmakey="kst%d" % s)

    BXT = Buf("xT_s")
    BY = Buf("y_s")

    def make_xT(b, xT, BxT, xst, Bxst, src, ntile, key):
        def xload(j):
            s = j % 2
            S.add("sp", lambda e: e.dma_start(out=xst[s][:], in_=src[b, sl(j, 128), :]), writes=[Bxst[s]], dmakey="%s%d" % (key, s))
        xload(0)
        for j in range(ntile):
            if j + 1 < ntile:
                xload(j + 1)
            s = j % 2
            for h in range(2):
                bank = 6 + h

                def tr(e, s=s, h=h, bank=bank):
                    for q in range(4):
                        ins = e.transpose(pb[bank][:, sl(q, 128)], xst[s][:, sl(h * 4 + q, 128)], identf[:])
                    return ins
                S.add("pe", tr, reads=[Bxst[s]] + CONST, writes=[PB[bank]])
                eng = "act" if h == 0 else "dve"
                if eng == "act":
                    S.add("act", lambda e, h=h, j=j, bank=bank: e.activation(xT[:, h * 4:(h + 1) * 4, sl(j, 128)], pb[bank][:].rearrange("p (q t) -> p q t", q=4), AF.Copy),
                          reads=[PB[bank]], writes=[BxT[j][h]])
                else:
                    S.add("dve", lambda e, h=h, j=j, bank=bank: e.tensor_copy(xT[:, h * 4:(h + 1) * 4, sl(j, 128)], pb[bank][:].rearrange("p (q t) -> p q t", q=4)),
                          reads=[PB[bank]], writes=[BxT[j][h]])

    def phase_H():
        S.fence()
        A.reset()
        tm = [A.alloc([128, 16, 1024], BF16) for _ in range(3)]
        mark = A.off
        xst = [A.alloc([128, D], F32) for _ in range(2)]
        xT = A.alloc([128, 8, L], BF16)
        wbf = [A.alloc([128, 8, 512], BF16) for _ in range(2)]
        upad = [A.alloc([128, L + 2], F32) for _ in range(2)]
        tmpc = [A.alloc([128, L], F32) for _ in range(2)]
        ubf = [A.alloc([128, L], BF16) for _ in range(2)]
        Bxst = [Buf("xst0"), Buf("xst1")]
        Bwbf = [Buf("wbf0"), Buf("wbf1")]
        Bupad = [Buf("upad0"), Buf("upad1")]
        Btmpc = [Buf("tmpc0"), Buf("tmpc1")]
        Bubf = [Buf("ubf0"), Buf("ubf1")]
        Btm = [[[[Buf("tm") for _ in range(4)] for _ in range(4)] for _ in range(NB)] for _ in range(3)]
        for s in range(2):
            S.add("dve", lambda e, s=s: e.memset(upad[s][:, 0:1], 0.0), writes=[Bupad[s]])
            S.add("dve", lambda e, s=s: e.memset(upad[s][:, L + 1:L + 2], 0.0), writes=[Bupad[s]])
        it = 0
        pendB = [None]
        BxT = [[Buf("xT"), Buf("xT")] for _ in range(16)]
        for b in range(NB):
            make_xT(b, xT, BxT, xst, Bxst, x, 16, "xst")
            allxT = [bb for pr in BxT for bb in pr]
            S.add("pool", lambda e, b=b: e.dma_start(out=xT_s[b], in_=xT[:]), reads=allxT, writes=[BXT], dmakey="xTst")
            for g in range(3):
                ws = (b * 3 + g) % 2
                S.add("sp", lambda e, g=g, ws=ws: e.dma_start(out=wbf[ws][:], in_=win_b[g]), reads=[BW["in"]], writes=[Bwbf[ws]], dmakey="w%d" % ws)
                for ci in range(4):
                    mt = g * 4 + ci
                    us = it % 2
                    it += 1
                    for c in range(4):
                        bank = c

                        def mm(e, ws=ws, ci=ci, c=c, bank=bank):
                            for k in range(8):
                                ins = e.matmul(pb[bank][:], wbf[ws][:, k, sl(ci, 128)], xT[:, k, sl(c, 512)], start=(k == 0), stop=(k == 7))
                            return ins
                        S.add("pe", mm, reads=[Bwbf[ws]] + [bb for j in range(4 * c, 4 * c + 4) for bb in BxT[j]], writes=[PB[bank]])
                        S.add("act", lambda e, us=us, c=c, bank=bank: e.activation(upad[us][:, 1 + c * 512:1 + (c + 1) * 512], pb[bank][:], AF.Copy),
                              reads=[PB[bank]], writes=[Bupad[us]])
                    S.add("act", lambda e, us=us, mt=mt: e.activation(tmpc[us][:], upad[us][:, 1:L + 1], AF.Identity, bias=hcb[:, mt:mt + 1], scale=hcw[:, mt, 1:2]),
                          reads=[Bupad[us]] + CONST, writes=[Btmpc[us]])
                    S.add("dve", lambda e, us=us, mt=mt: e.scalar_tensor_tensor(tmpc[us][:], upad[us][:, 0:L], hcw[:, mt, 0:1], tmpc[us][:], ALU.mult, ALU.add),
                          reads=[Bupad[us], Btmpc[us]] + CONST, writes=[Btmpc[us]])
                    S.add("dve", lambda e, us=us, mt=mt: e.scalar_tensor_tensor(ubf[us][:], upad[us][:, 2:L + 2], hcw[:, mt, 2:3], tmpc[us][:], ALU.mult, ALU.add),
                          reads=[Bupad[us], Btmpc[us]] + CONST, writes=[Bubf[us]])
                    def stageB(us=us, g=g, b=b, ci=ci):
                        for jg in range(4):
                            bank = 4 + (jg % 4)
                            pbb = pb[bank][:].bitcast(BF16)

                            def tr(e, us=us, jg=jg, pbb=pbb):
                                for q in range(4):
                                    ins = e.transpose(pbb[:, sl(q, 128)], ubf[us][:, sl(jg * 4 + q, 128)], identb[:])
                                return ins
                            S.add("pe", tr, reads=[Bubf[us]] + CONST, writes=[PB[bank]])
                            dst = tm[g][:, jg * 4:(jg + 1) * 4, b * 512 + ci * 128:b * 512 + (ci + 1) * 128]
                            src_ = pbb[:, 0:512].rearrange("p (q t) -> p q t", q=4)
                            if jg % 2 == 0:
                                S.add("act", lambda e, dst=dst, src_=src_: e.activation(dst, src_, AF.Copy), reads=[PB[bank]], writes=[Btm[g][b][jg][ci]])
                            else:
                                S.add("dve", lambda e, dst=dst, src_=src_: e.tensor_copy(dst, src_), reads=[PB[bank]], writes=[Btm[g][b][jg][ci]])
                    if pendB[0] is not None:
                        pendB[0]()
                    pendB[0] = stageB
        pendB[0]()
        if stop_after == "H1":
            return tm
        S.fence()
        A.reset(mark)
        Y = A.alloc([128, 16, 2, 1024], BF16)
        gcb = [A.alloc([128, 16, 128], BF16) for _ in range(2)]
        gsb = [A.alloc([128, 16, 128], BF16) for _ in range(2)]
        kl = [A.alloc([128, 2, 512], BF16) for _ in range(2)]
        zs = [A.alloc([128, 512], F32) for _ in range(2)]
        tt_ = [A.alloc([128, 512], F32) for _ in range(4)]
        tinv = [A.alloc([128, 512], F32) for _ in range(2)]
        hb = A.alloc([128, 2, 512], F32)
        ystg = [tt_[0][:].bitcast(BF16), tt_[1][:].bitcast(BF16)]
        Bhb = Buf("hb")
        S.add("sp", lambda e: e.dma_start(out=hb[:], in_=hb_d), writes=[Bhb], dmakey="c1H")
        Bg = [Buf("g0"), Buf("g1")]
        Bg2 = [Buf("gs0"), Buf("gs1")]
        Bkl = [Buf("kl0"), Buf("kl1")]
        Bzs = [Buf("zr"), Buf("zi")]
        Btt = [Buf("t%d" % i) for i in range(4)]
        Btinv = [Buf("tinv0"), Buf("tinv1")]
        Btm2 = [[[Buf("tm2") for _ in range(16)] for _ in range(NB)] for _ in range(3)]
        BYb = [[Buf("Y") for _ in range(NB)] for _ in range(16)]
        for o in range(2):
            zin = tm[0] if o == 0 else tm[1]
            zi_i = 0 if o == 0 else 1
            xg = tm[1] if o == 0 else tm[2]
            xg_i = 1 if o == 0 else 2

            def gload(m, inv):
                s = m % 2
                S.add("sp", lambda e: e.dma_start(out=gcb[s][:], in_=gc_d[m]), writes=[Bg[s]], dmakey="g%d" % s)
                S.add("sp", lambda e: e.dma_start(out=gsb[s][:], in_=(gsi_d if inv else gsf_d)[m]), writes=[Bg2[s]], dmakey="gs%d" % s)
                if not inv:
                    S.add("sp", lambda e, o=o: e.dma_start(out=kl[s][:], in_=kf_s[m, :, o, :, :]), reads=[BKF], writes=[Bkl[s]], dmakey="kl%d" % s)
            gload(0, False)
            for m in range(16):
                if m + 1 < 16:
                    gload(m + 1, False)
                s = m % 2
                for n in range(NB):
                    for part in range(2):
                        g = gcb[s] if part == 0 else gsb[s]
                        bank = part + 2 * ((m * NB + n) % 2)

                        def mmf(e, g=g, n=n, bank=bank, zin=zin):
                            for k in range(16):
                                ins = e.matmul(pb[bank][:], g[:, k, :], zin[:, k, sl(n, 512)], start=(k == 0), stop=(k == 15))
                            return ins
                        S.add("pe", mmf, reads=[Bg[s], Bg2[s]] + Btm2[zi_i][n], writes=[PB[bank]])
                        S.add("act", lambda e, part=part, bank=bank: e.activation(zs[part][:], pb[bank][:], AF.Copy), reads=[PB[bank]], writes=[Bzs[part]])
                    kr = kl[s][:, 0, :]
                    ki = kl[s][:, 1, :]
                    S.add("dve", lambda e, kr=kr: e.tensor_tensor(tt_[0][:], zs[0][:], kr, ALU.mult), reads=[Bzs[0], Bkl[s]], writes=[Btt[0]])
                    S.add("pool", lambda e, ki=ki: e.tensor_tensor(tt_[1][:], zs[1][:], ki, ALU.mult), reads=[Bzs[1], Bkl[s]], writes=[Btt[1]])
                    S.add("pool", lambda e, ki=ki: e.tensor_tensor(tt_[2][:], zs[0][:], ki, ALU.mult), reads=[Bzs[0], Bkl[s]], writes=[Btt[2]])
                    S.add("dve", lambda e, kr=kr: e.tensor_tensor(tt_[3][:], zs[1][:], kr, ALU.mult), reads=[Bzs[1], Bkl[s]], writes=[Btt[3]])
                    S.add("dve", lambda e, m=m, n=n: e.tensor_tensor(Y[:, m, 0, sl(n, 512)], tt_[0][:], tt_[1][:], ALU.subtract),
                          reads=[Btt[0], Btt[1]], writes=[BYb[m][n]])
                    S.add("dve", lambda e, m=m, n=n: e.tensor_tensor(Y[:, m, 1, sl(n, 512)], tt_[2][:], tt_[3][:], ALU.add),
                          reads=[Btt[2], Btt[3]], writes=[BYb[m][n]])
                    if m == 0:
                        S.add("dve", lambda e, n=n, kr=kr: e.scalar_tensor_tensor(Y[0:1, 0, 0, sl(n, 512)], zs[0][0:1, :], 0.5, kr[0:1, :], ALU.mult, ALU.mult),
                              reads=[Bzs[0], Bkl[s], BYb[m][n]], writes=[BYb[m][n]])
                        S.add("dve", lambda e, n=n, ki=ki: e.scalar_tensor_tensor(Y[0:1, 0, 1, sl(n, 512)], zs[1][0:1, :], 0.5, ki[0:1, :], ALU.mult, ALU.mult),
                              reads=[Bzs[1], Bkl[s], BYb[m][n]], writes=[BYb[m][n]])
            gload(0, True)
            for j in range(16):
                if j + 1 < 16:
                    gload(j + 1, True)
                s = j % 2
                for n in range(NB):
                    bank = 4 + (j * NB + n) % 4

                    def mmi(e, s=s, n=n, bank=bank):
                        for k in range(16):
                            e.matmul(pb[bank][:], gcb[s][:, k, :], Y[:, k, 0, sl(n, 512)], start=(k == 0), stop=False)
                            ins = e.matmul(pb[bank][:], gsb[s][:, k, :], Y[:, k, 1, sl(n, 512)], start=False, stop=(k == 15))
                        return ins
                    S.add("pe", mmi, reads=[Bg[s], Bg2[s]] + [BYb[k][n] for k in range(16)], writes=[PB[bank]])
                    ts_ = (j * NB + n) % 2
                    S.add("pool", lambda e, j=j, n=n, ts_=ts_, zin=zin, o=o: e.tensor_tensor(tinv[ts_][:], zin[:, j, sl(n, 512)], hb[:, o, :], ALU.mult),
                          reads=[Btm2[zi_i][n][j], Bhb], writes=[Btinv[ts_]])
                    S.add("dve", lambda e, bank=bank, ts_=ts_: e.scalar_tensor_tensor(tinv[ts_][:], pb[bank][:], 2.0 / NFFT, tinv[ts_][:], ALU.mult, ALU.add),
                          reads=[PB[bank], Btinv[ts_]], writes=[Btinv[ts_]])
                    S.add("dve", lambda e, j=j, n=n, ts_=ts_, xg=xg: e.tensor_tensor(xg[:, j, sl(n, 512)], tinv[ts_][:], xg[:, j, sl(n, 512)], ALU.mult),
                          reads=[Btinv[ts_], Btm2[xg_i][n][j]], writes=[Btm2[xg_i][n][j]])
        Bys = [Buf("ystg0"), Buf("ystg1")]
        iy = 0
        for b in range(NB):
            for ci in range(4):
                for half in range(2):
                    s = iy % 2
                    iy += 1
                    bank = s
                    pbb = pb[bank][:].bitcast(BF16)

                    def tr(e, b=b, ci=ci, half=half, pbb=pbb):
                        for q in range(8):
                            j = half * 8 + q
                            ins = e.transpose(pbb[:, sl(q, 128)], tm[2][:, j, b * 512 + ci * 128:b * 512 + (ci + 1) * 128], identb[:])
                        return ins
                    S.add("pe", tr, reads=[Btm2[2][b][jj] for jj in range(half * 8, half * 8 + 8)] + CONST, writes=[PB[bank]])
                    S.add("act", lambda e, s=s, pbb=pbb: e.activation(ystg[s], pbb, AF.Copy), reads=[PB[bank]] + Btt, writes=[Bys[s], Btt[s]])
                    S.add("pool", lambda e, s=s, b=b, ci=ci, half=half: e.dma_start(out=y_s[sl(ci, 128), b * L + half * 1024:b * L + (half + 1) * 1024], in_=ystg[s]),
                          reads=[Bys[s]], writes=[BY], dmakey="yst%d" % s)
        return tm

    def phase_D():
        S.fence()
        A.reset()
        xT = A.alloc([128, 8, L], BF16)
        ropec = A.alloc([128, L], F32)
        ropes = A.alloc([128, L], F32)
        qk = A.alloc([128, 12, L], BF16)
        Vt = A.alloc([128, 16, 512], BF16)
        wbf = [A.alloc([128, 8, 512], BF16) for _ in range(2)]
        mst = [A.alloc([128, D], F32) for _ in range(2)]
        memT = A.alloc([128, 8, 256], BF16)
        kmT = A.alloc([128, 4, 256], BF16)
        vm = A.alloc([128, 2, 512], BF16)
        mqT = [A.alloc([128, 512], BF16) for _ in range(2)]
        E = [A.alloc([128, 1024], BF16) for _ in range(4)]
        qf = [A.alloc([128, 512], F32) for _ in range(2)]
        r1_ = [A.alloc([128, 512], F32) for _ in range(2)]
        r2_ = [A.alloc([128, 512], F32) for _ in range(2)]
        epr = A.alloc([128, 1024], F32)
        ep = [epr[:, 0:512], epr[:, 512:1024]] + [A.alloc([128, 512], F32) for _ in range(5)]
        ystg = [A.alloc([128, L], BF16) for _ in range(2)]
        Brope = Buf("rope")
        Bwbf = [Buf("wbf0"), Buf("wbf1")]
        Bmst = [Buf("mst0"), Buf("mst1")]
        BmqT = [Buf("mq0"), Buf("mq1")]
        BE = [Buf("E%d" % i) for i in range(4)]
        Bqf = [Buf("qf0"), Buf("qf1")]
        Br1 = [Buf("r10"), Buf("r11")]
        Br2 = [Buf("r20"), Buf("r21")]
        Bep = [Buf("ep%d" % i) for i in range(7)]
        Bys = [Buf("ys0"), Buf("ys1")]
        wi = [0]
        iy = [0]

        def wload(src, rd):
            ws = wi[0] % 2
            wi[0] += 1
            S.add("sp", lambda e: e.dma_start(out=wbf[ws][:], in_=src), reads=[rd], writes=[Bwbf[ws]], dmakey="w%d" % ws)
            return ws

        BxT = Buf("xTd")
        BmT = [[Buf("mT"), Buf("mT")] for _ in range(2)]
        BkmT = Buf("kmT")
        Bvm = Buf("vm")
        BVt = [Buf("Vt") for _ in range(16)]
        Bqk = [[Buf("qk") for _ in range(4)] for _ in range(12)]
        S.add("pool", lambda e: e.memset(qk[64:128, 0:4, :], 0.0), writes=[bb for t_ in range(0, 4) for bb in Bqk[t_]])
        S.add("pool", lambda e: e.memset(qk[0:64, 4:8, :], 0.0), writes=[bb for t_ in range(4, 8) for bb in Bqk[t_]])
        for b in range(NB):
            S.add("sp", lambda e, b=b: e.dma_start(out=xT[:], in_=xT_s[b]), reads=[BXT], writes=[BxT], dmakey="xTld")
            make_xT(b, memT, BmT, mst, Bmst, mem, 2, "mst")
            allmT = [bb for pr in BmT for bb in pr]
            ws = wload(wkv_b[:, :, 0:512], BW["kv"])
            for h in range(4):
                bank = h % 2

                def mmk(e, ws=ws, h=h, bank=bank):
                    for k in range(8):
                        ins = e.matmul(pb[bank][:, 0:256], wbf[ws][:, k, sl(h, 128)], memT[:, k, :], start=(k == 0), stop=(k == 7))
                    return ins
                S.add("pe", mmk, reads=[Bwbf[ws]] + allmT, writes=[PB[bank]])
                S.add("act", lambda e, h=h, bank=bank: e.activation(kmT[:, h, :], pb[bank][:, 0:256], AF.Copy), reads=[PB[bank]], writes=[BkmT])
            ws = wload(wkv_b[:, :, 512:1024], BW["kv"])
            for mt in range(2):
                bank = 2 + mt

                def mmv(e, ws=ws, mt=mt, bank=bank):
                    for k in range(8):
                        ins = e.matmul(pb[bank][:], memT[:, k, sl(mt, 128)], wbf[ws][:, k, :], start=(k == 0), stop=(k == 7))
                    return ins
                S.add("pe", mmv, reads=[Bwbf[ws]] + allmT, writes=[PB[bank]])
                S.add("act", lambda e, mt=mt, bank=bank: e.activation(vm[:, mt, :], pb[bank][:], AF.Copy), reads=[PB[bank]], writes=[Bvm])
            ws = wload(win_b[6], BW["in"])
            items = [(h, c) for h in range(4) for c in range(4)]
            ysl = {}
            for h in range(4):
                ysl[h] = iy[0] % 2
                iy[0] += 1

            def stA(i, ws=ws):
                h, c = items[i]
                ms = i % 2
                bank = 0 if i % 2 == 0 else 5

                def mm(e, ws=ws, h=h, c=c, bank=bank):
                    for k in range(8):
                        ins = e.matmul(pb[bank][:], wbf[ws][:, k, sl(h, 128)], xT[:, k, sl(c, 512)], start=(k == 0), stop=(k == 7))
                    return ins
                S.add("pe", mm, reads=[Bwbf[ws], BxT], writes=[PB[bank]])
                S.add("dve", lambda e, ms=ms, bank=bank: e.tensor_copy(mqT[ms][:], pb[bank][:]), reads=[PB[bank]], writes=[BmqT[ms]])

            def stB(i):
                h, c = items[i]
                ms = i % 2
                for mt in range(2):
                    bank = (1 + mt) if i % 2 == 0 else (6 + mt)
                    es = 2 * (i % 2) + mt
                    S.add("pe", lambda e, h=h, mt=mt, ms=ms, bank=bank: e.matmul(pb[bank][:], kmT[:, h, sl(mt, 128)], mqT[ms][:], start=True, stop=True),
                          reads=[BkmT, BmqT[ms]], writes=[PB[bank]])
                    S.add("act", lambda e, es=es, bank=bank: e.activation(E[es][:, 0:512], pb[bank][:], AF.Exp, scale=128.0 ** -0.5),
                          reads=[PB[bank]], writes=[BE[es]])

            def stC(i):
                h, c = items[i]
                ys = ysl[h]
                e0 = 2 * (i % 2)

                def mmo(e, h=h, e0=e0):
                    e.matmul(pb[3][:], vm[:, 0, sl(h, 128)], E[e0][:, 0:512], start=True, stop=False)
                    e.matmul(pb[3][:], vm[:, 1, sl(h, 128)], E[e0 + 1][:, 0:512], start=False, stop=True)
                    e.matmul(pb[4][:], onesb[:], E[e0][:, 0:512], start=True, stop=False)
                    return e.matmul(pb[4][:], onesb[:], E[e0 + 1][:, 0:512], start=False, stop=True)
                S.add("pe", mmo, reads=[Bvm, BE[e0], BE[e0 + 1]] + CONST, writes=[PB[3], PB[4]])
                S.add("act", lambda e: e.activation(ep[0][:], pb[4][:], AF.Ln), reads=[PB[4]], writes=[Bep[0]])
                S.add("act", lambda e: e.activation(ep[0][:], ep[0][:], AF.Exp, scale=-1.0), reads=[Bep[0]], writes=[Bep[0]])
                S.add("dve", lambda e, ys=ys, c=c: e.tensor_tensor(ystg[ys][:, sl(c, 512)], pb[3][:], ep[0][:], ALU.mult),
                      reads=[PB[3], Bep[0]], writes=[Bys[ys]])
                if c == 3:
                    S.add("pool", lambda e, ys=ys, h=h, b=b: e.dma_start(out=y_s[sl(8 + h, 128), sl(b, L)], in_=ystg[ys][:]), reads=[Bys[ys]], writes=[BY], dmakey="yst%d" % ys)

            stA(0)
            stA(1)
            stB(0)
            for i in range(16):
                if i + 1 < 16:
                    stB(i + 1)
                stC(i)
                if i + 2 < 16:
                    stA(i + 2)
            ws = wload(win_b[5], BW["in"])
            for j in range(16):
                bank = j % 2

                def mmV(e, ws=ws, j=j, bank=bank):
                    for k in range(8):
                        ins = e.matmul(pb[bank][:], xT[:, k, sl(j, 128)], wbf[ws][:, k, :], start=(k == 0), stop=(k == 7))
                    return ins
                S.add("pe", mmV, reads=[Bwbf[ws], BxT], writes=[PB[bank]])
                if j % 2 == 0:
                    S.add("act", lambda e, j=j, bank=bank: e.activation(Vt[:, j, :], pb[bank][:], AF.Copy), reads=[PB[bank]], writes=[BVt[j]])
                else:
                    S.add("dve", lambda e, j=j, bank=bank: e.tensor_copy(Vt[:, j, :], pb[bank][:]), reads=[PB[bank]], writes=[BVt[j]])
            if b == 0:
                S.add("sp", lambda e: e.dma_start(out=ropec[:], in_=ropec_d), writes=[Brope], dmakey="c1D")
                S.add("sp", lambda e: e.dma_start(out=ropes[:], in_=ropes_d), writes=[Brope], dmakey="c1D")
            wsg = [wload(win_b[3], BW["in"]), wload(win_b[4], BW["in"])]
            qitems = [(g, h, c) for g in range(2) for h in range(4) for c in range(4)]

            def qA(i):
                g, h, c = qitems[i]
                s = i % 2
                bank = 2 + s
                ws = wsg[g]

                def mm(e, ws=ws, h=h, c=c, bank=bank):
                    for k in range(8):
                        ins = e.matmul(pb[bank][:], wbf[ws][:, k, sl(h, 128)], xT[:, k, sl(c, 512)], start=(k == 0), stop=(k == 7))
                    return ins
                S.add("pe", mm, reads=[Bwbf[ws], BxT], writes=[PB[bank]])
                S.add("act", lambda e, s=s, bank=bank: e.activation(qf[s][:], pb[bank][:], AF.Copy), reads=[PB[bank]], writes=[Bqf[s]])

            def qB(i):
                g, h, c = qitems[i]
                s = i % 2
                bank2 = 4 + s
                S.add("pe", lambda e, s=s, bank2=bank2: e.matmul(pb[bank2][:], prot[:], qf[s][:], start=True, stop=True),
                      reads=[Bqf[s]] + CONST, writes=[PB[bank2]])
                S.add("pool", lambda e, s=s, c=c: e.tensor_tensor(r1_[s][:], qf[s][:], ropec[:, sl(c, 512)], ALU.mult),
                      reads=[Bqf[s], Brope], writes=[Br1[s]])
                S.add("dve", lambda e, s=s, c=c, bank2=bank2: e.tensor_tensor(r2_[s][:], pb[bank2][:], ropes[:, sl(c, 512)], ALU.mult),
                      reads=[PB[bank2], Brope], writes=[Br2[s]])
                if g == 0:
                    S.add("dve", lambda e, s=s, c=c, h=h: e.tensor_tensor(qk[0:64, h, sl(c, 512)], r1_[s][0:64, :], r2_[s][0:64, :], ALU.add),
                          reads=[Br1[s], Br2[s]], writes=[Bqk[h][c]])
                    S.add("dve", lambda e, s=s, c=c, h=h: e.tensor_tensor(qk[64:128, 4 + h, sl(c, 512)], r1_[s][64:128, :], r2_[s][64:128, :], ALU.add),
                          reads=[Br1[s], Br2[s]], writes=[Bqk[4 + h][c]])
                else:
                    S.add("dve", lambda e, s=s, c=c, h=h: e.tensor_tensor(qk[:, 8 + h, sl(c, 512)], r1_[s][:], r2_[s][:], ALU.add),
                          reads=[Br1[s], Br2[s]], writes=[Bqk[8 + h][c]])

            qA(0)
            for i in range(len(qitems)):
                if i + 1 < len(qitems):
                    qA(i + 1)
                qB(i)
            ie = 0
            pend = [None]
            for h in range(4):
                ys = iy[0] % 2
                iy[0] += 1
                for qc in range(4):
                    def add_S(kt, h=h, qc=qc):
                        p_ = kt % 2

                        def mms(e, kt=kt, h=h, qc=qc, p_=p_):
                            e.matmul(pb[4 + 2 * p_][:], qk[:, 8 + h, sl(kt, 128)], qk[:, h, sl(qc, 512)], start=True, stop=True)
                            return e.matmul(pb[5 + 2 * p_][:], qk[:, 8 + h, sl(kt, 128)], qk[:, 4 + h, sl(qc, 512)], start=True, stop=True)
                        S.add("pe", mms, reads=[Bqk[8 + h][kt // 4], Bqk[h][qc], Bqk[4 + h][qc]], writes=[PB[4 + 2 * p_], PB[5 + 2 * p_]])
                    add_S(0)
                    for kt in range(16):
                        p_ = kt % 2
                        es = ie % 4
                        ie += 1
                        S.add("act", lambda e, es=es, p_=p_: e.activation(E[es][:].rearrange("p (c q) -> p c q", c=2), pbig[:, 4 + 2 * p_:6 + 2 * p_, :], AF.Exp, scale=0.125),
                              reads=[PB[4 + 2 * p_], PB[5 + 2 * p_]], writes=[BE[es]])
                        if kt + 1 < 16:
                            add_S(kt + 1)

                        def mmo(e, h=h, kt=kt, es=es):
                            e.matmul(pb[0][:], Vt[:, kt, sl(h, 128)], E[es][:, 0:512], start=(kt == 0), stop=(kt == 15))
                            e.matmul(pb[2][:], onesb[:], E[es][:, 0:512], start=(kt == 0), stop=(kt == 15))
                            e.matmul(pb[1][:], Vt[:, kt, sl(h, 128)], E[es][:, 512:1024], start=(kt == 0), stop=(kt == 15))
                            return e.matmul(pb[3][:], onesb[:], E[es][:, 512:1024], start=(kt == 0), stop=(kt == 15))
                        S.add("pe", mmo, reads=[BVt[kt], BE[es]] + CONST, writes=[PB[0], PB[1], PB[2], PB[3]])
                    prev = pend[0]
                    if prev is not None:
                        S.add("pe", lambda e: e.matmul(pb[6][:], onesf[:], ep[5][:], start=True, stop=True), reads=[Bep[5]] + CONST, writes=[PB[6]])
                    S.add("act", lambda e: e.activation(epr[:].rearrange("p (c q) -> p c q", c=2), pbig[:, 2:4, :], AF.Ln), reads=[PB[2], PB[3]], writes=[Bep[0], Bep[1]])
                    S.add("act", lambda e: e.activation(epr[:], epr[:], AF.Exp, scale=-1.0), reads=[Bep[0], Bep[1]], writes=[Bep[0], Bep[1]])
                    S.add("dve", lambda e: e.tensor_tensor(ep[2][:], pb[0][:], ep[0][:], ALU.mult), reads=[PB[0], Bep[0]], writes=[Bep[2]])
                    S.add("dve", lambda e: e.tensor_tensor(ep[3][:], pb[1][:], ep[1][:], ALU.mult), reads=[PB[1], Bep[1]], writes=[Bep[3]])
                    if prev is not None:
                        prev()
                    S.add("dve", lambda e: e.scalar_tensor_tensor(ep[4][:], ep[3][:], neglam[:, 0:1], ep[2][:], ALU.mult, ALU.add),
                          reads=[Bep[2], Bep[3]] + CONST, writes=[Bep[4]])
                    S.add("pool", lambda e: e.tensor_tensor(ep[5][:], ep[4][:], ep[4][:], ALU.mult), reads=[Bep[4]], writes=[Bep[5]])

                    def tail(ys=ys, qc=qc, h=h, b=b):
                        S.add("act", lambda e: e.activation(ep[6][:], pb[6][:], AF.Ln, bias=epsr[:, 0:1], scale=1.0 / 128.0), reads=[PB[6]] + CONST, writes=[Bep[6]])
                        S.add("act", lambda e: e.activation(ep[6][:], ep[6][:], AF.Exp, scale=-0.5), reads=[Bep[6]], writes=[Bep[6]])
                        S.add("dve", lambda e, ys=ys, qc=qc: e.scalar_tensor_tensor(ystg[ys][:, sl(qc, 512)], ep[4][:], gsc[:, 0:1], ep[6][:], ALU.mult, ALU.mult),
                              reads=[Bep[4], Bep[6]] + CONST, writes=[Bys[ys]])
                        if qc == 3:
                            S.add("pool", lambda e, ys=ys, h=h, b=b: e.dma_start(out=y_s[sl(4 + h, 128), sl(b, L)], in_=ystg[ys][:]), reads=[Bys[ys]], writes=[BY], dmakey="yst%d" % ys)
                    pend[0] = tail
            S.add("pe", lambda e: e.matmul(pb[6][:], onesf[:], ep[5][:], start=True, stop=True), reads=[Bep[5]] + CONST, writes=[PB[6]])
            pend[0]()
            pend[0] = None

    BX1 = Buf("x1_s")
    BX1T = Buf("x1T_s")
    BwdnG = Buf("wdn")
    BlnG = Buf("ln")
    sharedC = {}

    def layer_norm_tail(rr, Brr, mv, Bmv, ntt, lnp, gi, Bln, sdt, pns=None, addb_eng="pool", after_tile=None, defer=False, mulg_eng="pool"):
        def stats():
            S.add("dve", lambda e: e.tensor_scalar_add(sdt[:, 0, 0:ntt], mv[:, 0:ntt, 1], 1e-5), reads=[Bmv], writes=[Bmv])
            S.add("act", lambda e: e.activation(sdt[:, 1, 0:ntt], sdt[:, 0, 0:ntt], AF.Sqrt), reads=[Bmv], writes=[Bmv])
            S.add("dve", lambda e: e.reciprocal(sdt[:, 2, 0:ntt], sdt[:, 1, 0:ntt]), reads=[Bmv], writes=[Bmv])
            S.add("dve", lambda e: e.scalar_tensor_tensor(sdt[:, 3, 0:ntt], mv[:, 0:ntt, 0], -1.0, sdt[:, 2, 0:ntt], ALU.mult, ALU.mult), reads=[Bmv], writes=[Bmv])

        def tile_fn(tt):
            def f():
                pn = 128 if pns is None else pns[tt]
                S.add("act", lambda e: e.activation(rr[0:pn, tt, :], rr[0:pn, tt, :], AF.Identity, bias=sdt[0:pn, 3, tt:tt + 1], scale=sdt[0:pn, 2, tt:tt + 1]),
                      reads=[Bmv, Brr[tt]], writes=[Brr[tt]])
                S.add(mulg_eng, lambda e: e.tensor_tensor(rr[0:pn, tt, :], rr[0:pn, tt, :], lnp[0:pn, gi, :], ALU.mult), reads=[Brr[tt], Bln], writes=[Brr[tt]])
                S.add(addb_eng, lambda e: e.tensor_tensor(rr[0:pn, tt, :], rr[0:pn, tt, :], lnp[0:pn, gi + 1, :], ALU.add), reads=[Brr[tt], Bln], writes=[Brr[tt]])
                if after_tile is not None:
                    after_tile(tt)
            return f
        fns = [stats] + [tile_fn(tt) for tt in range(ntt)]
        if defer:
            return fns
        for f in fns:
            f()

    def residual_stats(rr, Brr, tt, xres, Bxres, banks, mv, Bmv, bst, Bbst, pn=128):
        for nh in range(2):
            S.add("dve", lambda e, nh=nh: e.scalar_tensor_tensor(rr[0:pn, tt, sl(nh, 512)], xres[0:pn, sl(nh, 512)], ALPHA, pb[banks[nh]][0:pn, :], ALU.mult, ALU.add),
                  reads=[Bxres, PB[banks[nh]]], writes=[Brr[tt]])
        for nh in range(2):
            S.add("dve", lambda e, nh=nh: e.bn_stats(bst[0:pn, nh, :], rr[0:pn, tt, sl(nh, 512)]), reads=[Brr[tt], Bbst], writes=[Bbst])
        S.add("dve", lambda e: e.bn_aggr(mv[0:pn, tt, :], bst[0:pn]), reads=[Bbst, Bmv], writes=[Bmv])

    def phase_C1():
        S.fence()
        A.reset()
        wdn = A.alloc([128, 22, D], BF16)
        lnp = A.alloc([128, 4, D], F32)
        sharedC["wdn"], sharedC["lnp"], sharedC["off"] = wdn, lnp, A.off
        wo = A.alloc([128, 12, D], BF16)
        ych = [A.alloc([128, 12, 512], BF16) for _ in range(2)]
        xst = [A.alloc([128, D], F32) for _ in range(2)]
        rr = [A.alloc([128, 4, D], F32) for _ in range(2)]
        x1T = [A.alloc([128, 8, 512], BF16) for _ in range(2)]
        mv = [A.alloc([128, 4, 2], F32) for _ in range(2)]
        sdt = [A.alloc([128, 4, 4], F32) for _ in range(2)]
        bst = A.alloc([128, 2, 6], F32)
        Bwo = Buf("wo")
        Bln = BlnG
        S.add("sp", lambda e: e.dma_start(out=wo[:], in_=wout_b), reads=[BW["out"]], writes=[Bwo], dmakey="c1C1")
        S.add("sp", lambda e: e.dma_start(out=lnp[:], in_=ln_d), writes=[Bln], dmakey="c1C1")
        Bych = [Buf("ych0"), Buf("ych1")]
        Bxst = [Buf("xst0"), Buf("xst1")]
        Bx1T = [Buf("x1T0"), Buf("x1T1")]
        Bbst = Buf("bst")
        ixc = [0]
        Brr_all = [[Buf("rr") for _ in range(4)] for _ in range(2)]
        Bmv_all = [Buf("mv0"), Buf("mv1")]

        def stage1_begin(cg):
            cs = cg % 2
            S.add("sp", lambda e, cg=cg, cs=cs: e.dma_start(out=ych[cs][:], in_=y_s[:, sl(cg, 512)].rearrange("(k p) t -> p k t", p=128)),
                  reads=[BY], writes=[Bych[cs]], dmakey="yl%d" % cs)

        def stage1_tile(cg, tt):
            b, c = cg // 4, cg % 4
            cs = cg % 2
            Brr = Brr_all[cs]
            Bmv = Bmv_all[cs]
            xs_ = ixc[0] % 2
            ixc[0] += 1
            tok0 = c * 512 + tt * 128
            S.add("sp", lambda e, xs_=xs_, b=b, tok0=tok0: e.dma_start(out=xst[xs_][:], in_=x[b, tok0:tok0 + 128, :]), writes=[Bxst[xs_]], dmakey="xst%d" % xs_)
            banks = [(tt % 2) * 2, (tt % 2) * 2 + 1]
            for nh in range(2):
                def mm(e, cs=cs, tt=tt, nh=nh, bank=banks[nh]):
                    for k in range(12):
                        ins = e.matmul(pb[bank][:], ych[cs][:, k, sl(tt, 128)], wo[:, k, sl(nh, 512)], start=(k == 0), stop=(k == 11))
                    return ins
                S.add("pe", mm, reads=[Bych[cs], Bwo], writes=[PB[banks[nh]]])
            residual_stats(rr[cs], Brr, tt, xst[xs_], Bxst[xs_], banks, mv[cs], Bmv, bst, Bbst)

        def stage2_fns(cg):
            cs = cg % 2
            Brr = Brr_all[cs]

            def store(tt, cs=cs, cg=cg, Brr=Brr):
                g0 = cg * 512 + tt * 128
                S.add("pool", lambda e: e.dma_start(out=x1_s[g0:g0 + 128, :], in_=rr[cs][:, tt, :]), reads=[Brr[tt]], writes=[BX1], dmakey="x1st%d" % cs)
            return layer_norm_tail(rr[cs], Brr, mv[cs], Bmv_all[cs], 4, lnp, 0, Bln, sdt[cs], addb_eng="dve", mulg_eng="dve",
                                   after_tile=store, defer=True)

        def stage2_pe(cg, tt):
            cs = cg % 2
            Brr = Brr_all[cs]
            for h in range(2):
                bank = 4 + h + 2 * (tt % 2)

                def tr(e, cs=cs, tt=tt, h=h, bank=bank):
                    for q in range(4):
                        ins = e.transpose(pb[bank][:, sl(q, 128)], rr[cs][:, tt, sl(h * 4 + q, 128)], identf[:])
                    return ins
                S.add("pe", tr, reads=[Brr[tt]] + CONST, writes=[PB[bank]])
                dst = x1T[cs][:, h * 4:(h + 1) * 4, sl(tt, 128)]
                src_ = pb[bank][:].rearrange("p (q t) -> p q t", q=4)
                if h == 0:
                    S.add("act", lambda e, dst=dst, src_=src_: e.activation(dst, src_, AF.Copy), reads=[PB[bank]], writes=[Bx1T[cs]])
                else:
                    S.add("dve", lambda e, dst=dst, src_=src_: e.tensor_copy(dst, src_), reads=[PB[bank]], writes=[Bx1T[cs]])
            if tt == 3:
                S.add("pool", lambda e, cs=cs, cg=cg: e.dma_start(out=x1T_s[:, :, sl(cg, 512)], in_=x1T[cs][:]), reads=[Bx1T[cs]], writes=[BX1T], dmakey="x1Tst%d" % cs)

        stage1_begin(0)
        for tt in range(4):
            stage1_tile(0, tt)
        S.add("sp", lambda e: e.dma_start(out=wdn[:], in_=wdn_b), reads=[BW["down"]], writes=[BwdnG], dmakey="c1C1w")
        for cg in range(1, 9):
            fns = stage2_fns(cg - 1)
            if cg < 8:
                stage1_begin(cg)
            fns[0]()
            for tt in range(4):
                fns[1 + tt]()
                if cg < 8:
                    stage1_tile(cg, tt)
                stage2_pe(cg - 1, tt)


    def phase_C2():
        S.fence()
        A.reset(sharedC["off"])
        wdn, lnp = sharedC["wdn"], sharedC["lnp"]
        xw = [A.alloc([128, 8, 512], BF16) for _ in range(2)]
        wup = [A.alloc([128, 8, 2, 128], BF16) for _ in range(3)]
        t1 = [A.alloc([128, 512], F32) for _ in range(4)]
        hh = [A.alloc([128, 512], F32) for _ in range(4)]
        sg = [A.alloc([128, 512], F32) for _ in range(2)]
        act = A.alloc([128, 22, 512], BF16)
        x1t = [A.alloc([128, D], F32) for _ in range(2)]
        rr = [A.alloc([128, 4, D], F32) for _ in range(2)]
        mv = [A.alloc([128, 4, 2], F32) for _ in range(2)]
        sdt = [A.alloc([128, 4, 4], F32) for _ in range(2)]
        bst = A.alloc([128, 2, 6], F32)
        Bwdn = BwdnG
        Bln = BlnG
        Bxw = [Buf("xw0"), Buf("xw1")]
        Bwup = [Buf("wup%d" % i) for i in range(3)]
        Bt1 = [Buf("t1%d" % i) for i in range(4)]
        Bhh = [Buf("hh%d" % i) for i in range(4)]
        Bsg = [Buf("sg0"), Buf("sg1")]
        Bx1t = [Buf("x1t0"), Buf("x1t1")]
        Bbst = Buf("bst")
        Bmv_all = [Buf("mv0"), Buf("mv1")]
        for i_ in range(2):
            S.add("pool", lambda e, i_=i_: e.memset(mv[i_][:], 1.0), writes=[Bmv_all[i_]])
            S.add("pool", lambda e, i_=i_: e.memset(sdt[i_][:], 1.0), writes=[Bmv_all[i_]])
        iw = [0]

        def wuload(m):
            s = iw[0] % 3
            iw[0] += 1
            S.add("sp", lambda e: e.dma_start(out=wup[s][:], in_=wup_b[m]), reads=[BW["up"]], writes=[Bwup[s]], dmakey="wu%d" % s)
            return s
        ig = 0
        ix = 0
        Bact = [Buf("act") for _ in range(22)]
        Brr_all = [[Buf("rr") for _ in range(4)] for _ in range(2)]
        chunks = []
        for b in range(NB):
            for c0, n_ in ((0, 510), (510, 510), (1020, 510), (1530, 262), (1792, 256)):
                chunks.append((b, c0, n_))
        pendC = []
        for cgi, (b, c0, n) in enumerate(chunks):
            cs = cgi % 2
            g0 = b * L + c0
            N = n + 2
            lo = 1 if c0 == 0 else 0
            hi = n + 1 if c0 + n == L else n + 2
            S.add("sp", lambda e, cs=cs, g0=g0, lo=lo, hi=hi: e.dma_start(out=xw[cs][:, :, lo:hi], in_=x1T_s[:, :, g0 - 1 + lo:g0 - 1 + hi]),
                  reads=[BX1T], writes=[Bxw[cs]], dmakey="xw%d" % cs)
            if c0 == 0:
                S.add("pool", lambda e, cs=cs: e.memset(xw[cs][:, :, 0:1], 0.0), writes=[Bxw[cs]])
            if c0 + n == L:
                S.add("pool", lambda e, cs=cs, n=n: e.memset(xw[cs][:, :, n + 1:n + 2], 0.0), writes=[Bxw[cs]])
            nxt = wuload(0)
            for m in range(22):
                ws = nxt
                if m + 1 < 22:
                    nxt = wuload(m + 1)
                hsl = []
                for gu in range(2):
                    s = ig % 4
                    ig += 1
                    bank = s

                    def mm(e, ws=ws, gu=gu, cs=cs, bank=bank, N=N):
                        for k in range(8):
                            ins = e.matmul(pb[bank][:, 0:N], wup[ws][:, k, gu, :], xw[cs][:, k, 0:N], start=(k == 0), stop=(k == 7))
                        return ins
                    S.add("pe", mm, reads=[Bwup[ws], Bxw[cs]], writes=[PB[bank]])
                    fi = gu * 22 + m
                    S.add("act", lambda e, s=s, fi=fi, bank=bank, n=n: e.activation(t1[s][:, 0:n], pb[bank][:, 1:n + 1], AF.Identity, bias=fcb[:, fi:fi + 1], scale=fcw[:, fi, 1:2]),
                          reads=[PB[bank]] + CONST, writes=[Bt1[s]])
                    S.add("dve", lambda e, s=s, fi=fi, bank=bank, n=n: e.scalar_tensor_tensor(t1[s][:, 0:n], pb[bank][:, 0:n], fcw[:, fi, 0:1], t1[s][:, 0:n], ALU.mult, ALU.add),
                          reads=[PB[bank], Bt1[s]] + CONST, writes=[Bt1[s]])
                    S.add("dve", lambda e, s=s, fi=fi, bank=bank, n=n: e.scalar_tensor_tensor(hh[s][:, 0:n], pb[bank][:, 2:n + 2], fcw[:, fi, 2:3], t1[s][:, 0:n], ALU.mult, ALU.add),
                          reads=[PB[bank], Bt1[s]] + CONST, writes=[Bhh[s]])
                    hsl.append(s)
                if pendC and m in (2, 6, 10, 14, 18):
                    pendC.pop(0)()
                ss = m % 2
                S.add("act", lambda e, ss=ss, s0=hsl[0], n=n: e.activation(sg[ss][:, 0:n], hh[s0][:, 0:n], AF.Silu), reads=[Bhh[hsl[0]]], writes=[Bsg[ss]])
                S.add("pool", lambda e, ss=ss, s1=hsl[1], m=m, n=n: e.tensor_tensor(act[:, m, 0:n], sg[ss][:, 0:n], hh[s1][:, 0:n], ALU.mult),
                      reads=[Bsg[ss], Bhh[hsl[1]]], writes=[Bact[m]])
            Brr = Brr_all[cs]
            Bmv = Bmv_all[cs]
            tiles = [(ts, min(128, n - ts)) for ts in range(0, n, 128)]
            for tt, (ts, tn) in enumerate(tiles):
                xs_ = ix % 2
                ix += 1
                S.add("sp", lambda e, xs_=xs_, g0=g0, ts=ts, tn=tn: e.dma_start(out=x1t[xs_][0:tn, :], in_=x1_s[g0 + ts:g0 + ts + tn, :]),
                      reads=[BX1], writes=[Bx1t[xs_]], dmakey="x1l%d" % xs_)
                banks = [6, 7] if tt % 2 == 0 else [4, 5]
                for nh in range(2):
                    def mm1(e, ts=ts, tn=tn, nh=nh, bank=banks[nh]):
                        for k in range(16):
                            ins = e.matmul(pb[bank][0:tn, :], act[:, k, ts:ts + tn], wdn[:, k, sl(nh, 512)], start=(k == 0), stop=False)
                        return ins

                    def mm2(e, ts=ts, tn=tn, nh=nh, bank=banks[nh]):
                        for k in range(16, 22):
                            ins = e.matmul(pb[bank][0:tn, :], act[:, k, ts:ts + tn], wdn[:, k, sl(nh, 512)], start=False, stop=(k == 21))
                        return ins
                    S.add("pe", mm1, reads=Bact[0:16] + [Bwdn], writes=[PB[banks[nh]]])
                    S.add("pe", mm2, reads=Bact[16:22], writes=[PB[banks[nh]]])
                residual_stats(rr[cs], Brr, tt, x1t[xs_], Bx1t[xs_], banks, mv[cs], Bmv, bst, Bbst, pn=tn)
            while pendC:
                pendC.pop(0)()

            def store_tile(tt, cs=cs, b=b, c0=c0, tiles=tiles, Brr=Brr):
                ts, tn = tiles[tt]
                finals.append(S.add("pool", lambda e: e.dma_start(out=out[b, c0 + ts:c0 + ts + tn, :], in_=rr[cs][0:tn, tt, :]),
                                    reads=[Brr[tt]], dmakey="ost%d" % cs))
            pendC.extend(layer_norm_tail(rr[cs], Brr, mv[cs], Bmv, len(tiles), lnp, 2, Bln, sdt[cs], pns=[tn for _, tn in tiles],
                                         after_tile=store_tile, defer=True))
        while pendC:
            pendC.pop(0)()

    phase_A()
    phase_W()
    if stop_after != "A":
        phase_H()
        if stop_after not in ("H1", "H"):
            phase_D()
            if stop_after != "D":
                phase_C1()
                if stop_after != "C1":
                    phase_C2()
    if not finals:
        finals.extend(S.last.values())
    S.emit(final_waits=finals)
    return nc


_CONST = None


def _constants():
    global _CONST
    if _CONST is not None:
        return _CONST
    bf = ml_dtypes.bfloat16
    c = {}
    c["identf"] = np.eye(128, dtype=np.float32)
    c["identb"] = np.eye(128).astype(bf)
    P = np.zeros((128, 128), np.float32)
    for m in range(128):
        i = m % 64
        if i < 32:
            P[m + 32, m] = -1.0
        else:
            P[m - 32, m] = 1.0
    c["prot"] = P
    inv_freq = (10000.0 ** (-np.arange(0, 64, 2, dtype=np.float32) / 64)).astype(np.float32)
    ang = np.arange(L, dtype=np.float32)[:, None] * inv_freq[None, :]
    ang = np.concatenate([ang, ang], -1)
    cosT = np.cos(ang).astype(np.float32).T
    sinT = np.sin(ang).astype(np.float32).T
    c["ropec"] = np.ascontiguousarray(np.concatenate([cosT, cosT], 0))
    c["ropes"] = np.ascontiguousarray(np.concatenate([sinT, sinT], 0))
    t = np.linspace(0.0, 1.0, L, dtype=np.float32)[:, None]
    fr = np.linspace(1e-4, 15, 16, dtype=np.float32)[None, :]
    w = (2.0 * math.pi * np.arange(L, dtype=np.float32)[:, None] / L).astype(np.float32)
    z = np.concatenate([t, np.cos(fr * w), -np.sin(fr * w)], -1).astype(np.float32)
    c["zT"] = np.ascontiguousarray(z.T)
    dmin = math.log(1e-2) / 0.3
    dmax = math.log(1e-2) / 1.5
    deltas = np.abs(np.linspace(dmin, dmax, HW, dtype=np.float32))
    decay = np.exp(-t * deltas[None, :]).astype(np.float32)
    c["decay"] = np.ascontiguousarray(decay.reshape(16, 128, HW).transpose(1, 0, 2))
    n = np.arange(2048, dtype=np.int64)
    prod = (n[:, None] * n[None, :]) % NFFT
    angm = 2.0 * np.pi * prod.astype(np.float64) / NFFT
    Gc = np.cos(angm)
    Gs = -np.sin(angm)
    GsF = Gs.copy()
    GsF[:, 0] = (-1.0) ** n
    GsI = Gs.copy()
    GsI[0, :] = (-1.0) ** n

    def tile4(G):
        return np.ascontiguousarray(G.reshape(16, 128, 16, 128).transpose(2, 1, 0, 3)).astype(bf)
    c["gc"] = tile4(Gc)
    c["gsf"] = tile4(GsF)
    c["gsi"] = tile4(GsI)
    _CONST = c
    return c


def _prep_shared(inp):
    f = np.float32
    d = {}
    d["w_in"] = np.ascontiguousarray(inp["w_in"][0], f)
    d["w_kv"] = np.ascontiguousarray(inp["mem_w_kv"][0], f)
    d["w_out"] = np.ascontiguousarray(inp["w_out"][0], f)
    d["w_up"] = np.ascontiguousarray(inp["ffn_w_up"][0], f)
    d["w_down"] = np.ascontiguousarray(inp["ffn_w_down"][0], f)
    d["hcw"] = np.ascontiguousarray(np.asarray(inp["hy_conv_w"][0], f).reshape(3, 12, 128).transpose(2, 1, 0))
    d["hcb"] = np.ascontiguousarray(np.asarray(inp["hy_conv_b"][0], f).reshape(12, 128).T)
    d["fcw"] = np.ascontiguousarray(np.asarray(inp["ffn_conv_w"][0], f).reshape(3, 44, 128).transpose(2, 1, 0))
    d["fcb"] = np.ascontiguousarray(np.asarray(inp["ffn_conv_b"][0], f).reshape(44, 128).T)
    d["w1"] = np.ascontiguousarray(inp["hy_w1"][0], f)
    d["w2"] = np.ascontiguousarray(inp["hy_w2"][0], f)
    d["w3"] = np.ascontiguousarray(inp["hy_w3"][0], f)
    d["fv"] = np.ascontiguousarray(np.stack([inp["hy_b1"][0], inp["hy_b2"][0], inp["hy_freq"][0]], -1), f)
    d["hbias"] = np.ascontiguousarray(np.broadcast_to(np.asarray(inp["hy_bias"][0], f)[None], (128, 2, 512)))
    d["lam"] = np.ascontiguousarray(np.broadcast_to(np.asarray(inp["diff_lambda"][0], f).reshape(1, 256), (128, 256)))
    d["subg"] = np.ascontiguousarray(np.asarray(inp["diff_subln_g"][0], f).reshape(128, 1))
    lnp = np.stack([inp["ln1_g"][0], inp["ln1_b"][0], inp["ln2_g"][0], inp["ln2_b"][0]], 0).astype(f)
    d["lnp"] = np.ascontiguousarray(np.broadcast_to(lnp[None], (128, 4, 1024)))
    d.update(_constants())
    return d


_NC_CACHE = {}


def kernel(**inputs):
    inp = {k: np.asarray(v) for k, v in inputs.items()}
    shared = _prep_shared(inp)
    x = np.ascontiguousarray(inp["x"], np.float32)
    mem = np.ascontiguousarray(inp["mem"], np.float32)
    if "nc" not in _NC_CACHE:
        _NC_CACHE["nc"] = build()
    nc = _NC_CACHE["nc"]
    in_maps = []
    for i in range(NCORES):
        m = dict(shared)
        m["x"] = x[i * NB:(i + 1) * NB]
        m["mem"] = mem[i * NB:(i + 1) * NB]
        in_maps.append(m)
    res = run_bass_kernel_spmd(nc, in_maps, core_ids=list(range(NCORES)))
    outs = [np.asarray(r["out"], np.float32) for r in res.results]
    return np.concatenate(outs, axis=0)
```
